# Optimizing a Trainium2 kernel written in Bass

```python
import jax, jax.numpy as jnp
from jax import lax
import numpy as np

D_MODEL = 1024
BATCH = 2
SEQ = 8192
DEPTH = 2

GRID_W = 64
CTX_LEN = 256
N_MIXERS = 2
N_POOL_GROUPS = 4
POOL_WINDOWS = (2, 4, 8, 16)
POOL_GROUP_DIM = D_MODEL // N_POOL_GROUPS
HEAD_DIM = 128
N_Q_HEADS = D_MODEL // HEAD_DIM
N_KV_HEADS = 2
Q_PER_KV = N_Q_HEADS // N_KV_HEADS
QKV_DIM = (N_Q_HEADS + 2 * N_KV_HEADS) * HEAD_DIM
ROPE_THETA = 10000.0
Q_BLOCK = 128
D_FF = 4 * D_MODEL
N_MOD = 6
EPS = 1e-6
N_POOL_LAYERS = (DEPTH + 1) // 2
N_ATTN_LAYERS = DEPTH // 2

kernel_name = 'hybrid_pool_attn_dit_block'


def rms_norm(x, g):
    xf = x.astype(jnp.float32)
    y = xf * lax.rsqrt(jnp.mean(xf * xf, axis=-1, keepdims=True) + EPS)
    return (y * g.astype(jnp.float32)).astype(x.dtype)


def ada_mods(cond, w, b):
    m = (jax.nn.silu(cond) @ w + b)[..., None, :]
    return jnp.split(m, N_MOD, axis=-1)


def modulate(h, shift, scale):
    return h * (1.0 + scale) + shift


def pool_mixer(h, w_pool, pool_scale):
    B, L, D = h.shape
    t = jnp.arange(L)
    hf = h.astype(jnp.float32)
    cs = jnp.concatenate([jnp.zeros((B, 1, D), jnp.float32), jnp.cumsum(hf, axis=1)], axis=1)
    outs = []
    for g, w in enumerate(POOL_WINDOWS):
        sl = slice(g * POOL_GROUP_DIM, (g + 1) * POOL_GROUP_DIM)
        lo = jnp.clip(t - w // 2, 0, L)
        hi = jnp.clip(t + w - w // 2, 0, L)
        csg = cs[..., sl]
        cnt = (hi - lo).astype(jnp.float32)[:, None]
        mean = (jnp.take(csg, hi, axis=1) - jnp.take(csg, lo, axis=1)) / cnt
        diff = (mean - hf[..., sl]).astype(h.dtype)
        outs.append(diff @ w_pool[g])
    return jnp.concatenate(outs, axis=-1) * pool_scale


def axial_rope_tables(L):
    rows = L // GRID_W
    row = jnp.broadcast_to(jnp.arange(rows)[:, None], (rows, GRID_W)).reshape(L).astype(jnp.float32)
    col = jnp.broadcast_to(jnp.arange(GRID_W)[None, :], (rows, GRID_W)).reshape(L).astype(jnp.float32)
    half = HEAD_DIM // 2
    inv_freq = jnp.power(jnp.float32(ROPE_THETA), -jnp.arange(0, half, 2, dtype=jnp.float32) / half)
    ang_r = row[:, None] * inv_freq
    ang_c = col[:, None] * inv_freq
    return (jnp.cos(ang_r)[:, None, :], jnp.sin(ang_r)[:, None, :],
            jnp.cos(ang_c)[:, None, :], jnp.sin(ang_c)[:, None, :])


def _rotate(x, cos, sin):
    x1, x2 = jnp.split(x, 2, axis=-1)
    return jnp.concatenate([x1 * cos - x2 * sin, x2 * cos + x1 * sin], axis=-1)


def apply_axial_rope(x, tables):
    cr, sr, cc, sc = tables
    xf = x.astype(jnp.float32)
    half = HEAD_DIM // 2
    out = jnp.concatenate([_rotate(xf[..., :half], cr, sr), _rotate(xf[..., half:], cc, sc)], axis=-1)
    return out.astype(x.dtype)


def qkv_project(h, w_qkv, g_q, g_k):
    B, L, _ = h.shape
    qkv = h @ w_qkv
    q = qkv[..., :N_Q_HEADS * HEAD_DIM].reshape(B, L, N_Q_HEADS, HEAD_DIM)
    k = qkv[..., N_Q_HEADS * HEAD_DIM:(N_Q_HEADS + N_KV_HEADS) * HEAD_DIM].reshape(B, L, N_KV_HEADS, HEAD_DIM)
    v = qkv[..., (N_Q_HEADS + N_KV_HEADS) * HEAD_DIM:].reshape(B, L, N_KV_HEADS, HEAD_DIM)
    return rms_norm(q, g_q), rms_norm(k, g_k), v


def attend(q, k, v):
    s = jnp.einsum('bqhgd,bkhd->bhgqk', q, k).astype(jnp.float32) * (HEAD_DIM ** -0.5)
    p = jax.nn.softmax(s, axis=-1).astype(v.dtype)
    return jnp.einsum('bhgqk,bkhd->bqhgd', p, v)


def attention_mixer(h_lat, h_ctx, w_qkv, g_q, g_k, w_o, need_ctx_out):
    B, L, D = h_lat.shape
    C = h_ctx.shape[1]
    tables = axial_rope_tables(L)
    q_l, k_l, v_l = qkv_project(h_lat, w_qkv, g_q, g_k)
    q_l = apply_axial_rope(q_l, tables)
    k_l = apply_axial_rope(k_l, tables)
    q_c, k_c, v_c = qkv_project(h_ctx, w_qkv, g_q, g_k)
    k_all = jnp.concatenate([k_c, k_l], axis=1)
    v_all = jnp.concatenate([v_c, v_l], axis=1)
    nb = L // Q_BLOCK
    qb = q_l.reshape(B, nb, Q_BLOCK, N_KV_HEADS, Q_PER_KV, HEAD_DIM).transpose(1, 0, 2, 3, 4, 5)
    o_blocks = lax.map(lambda qblk: attend(qblk, k_all, v_all), qb)
    o_lat = o_blocks.transpose(1, 0, 2, 3, 4, 5).reshape(B, L, D) @ w_o
    o_ctx = None
    if need_ctx_out:
        qc = q_c.reshape(B, C, N_KV_HEADS, Q_PER_KV, HEAD_DIM)
        o_ctx = attend(qc, k_c, v_c).reshape(B, C, D) @ w_o
    return o_lat, o_ctx


def sq_relu_mlp(h, w_in, w_out):
    return jnp.square(jax.nn.relu(h @ w_in)) @ w_out


def setup_inputs(seed: int = 0) -> dict:
    key = jax.random.key(seed)
    ks = jax.random.split(key, 20)
    f32 = jnp.float32
    nrm = lambda k, shape: jax.random.normal(k, shape, f32)
    gain = lambda k, shape: 1.0 + 0.05 * nrm(k, shape)
    return {
        'x': nrm(ks[0], (BATCH, SEQ, D_MODEL)),
        'c': nrm(ks[1], (BATCH, D_MODEL)),
        'ctx': nrm(ks[2], (BATCH, CTX_LEN, D_MODEL)),
        'c_ctx': nrm(ks[3], (D_MODEL,)),
        'w_ada': nrm(ks[4], (DEPTH, D_MODEL, N_MOD * D_MODEL)) * (0.5 * D_MODEL ** -0.5),
        'b_ada': 0.01 * nrm(ks[5], (DEPTH, N_MOD * D_MODEL)),
        'g_mix_pre': gain(ks[6], (DEPTH, D_MODEL)),
        'g_mix_post': gain(ks[7], (DEPTH, D_MODEL)),
        'g_mlp_pre': gain(ks[8], (DEPTH, D_MODEL)),
        'g_mlp_post': gain(ks[9], (DEPTH, D_MODEL)),
        'w_pool': nrm(ks[10], (N_POOL_LAYERS, N_POOL_GROUPS, POOL_GROUP_DIM, POOL_GROUP_DIM)) * POOL_GROUP_DIM ** -0.5,
        'pool_scale': 1.0 + 0.1 * nrm(ks[11], (N_POOL_LAYERS, D_MODEL)),
        'w_qkv': nrm(ks[12], (N_ATTN_LAYERS, D_MODEL, QKV_DIM)) * D_MODEL ** -0.5,
        'g_q': gain(ks[13], (N_ATTN_LAYERS, HEAD_DIM)),
        'g_k': gain(ks[14], (N_ATTN_LAYERS, HEAD_DIM)),
        'w_o': nrm(ks[15], (N_ATTN_LAYERS, D_MODEL, D_MODEL)) * D_MODEL ** -0.5,
        'w_mlp_in': nrm(ks[16], (DEPTH, D_MODEL, D_FF)) * D_MODEL ** -0.5,
        'w_mlp_out': nrm(ks[17], (DEPTH, D_FF, D_MODEL)) * D_FF ** -0.5,
    }


def reference(x, c, ctx, c_ctx, w_ada, b_ada, g_mix_pre, g_mix_post, g_mlp_pre, g_mlp_post,
              w_pool, pool_scale, w_qkv, g_q, g_k, w_o, w_mlp_in, w_mlp_out):
    for i in range(DEPTH):
        last = i == DEPTH - 1
        j = i // N_MIXERS
        sh1, sc1, gt1, sh2, sc2, gt2 = ada_mods(c, w_ada[i], b_ada[i])
        csh1, csc1, cgt1, csh2, csc2, cgt2 = ada_mods(c_ctx, w_ada[i], b_ada[i])
        h_lat = modulate(rms_norm(x, g_mix_pre[i]), sh1, sc1)
        h_ctx = modulate(rms_norm(ctx, g_mix_pre[i]), csh1, csc1)
        if i % N_MIXERS == 0:
            y_lat = pool_mixer(h_lat, w_pool[j], pool_scale[j])
            y_ctx = None if last else pool_mixer(h_ctx, w_pool[j], pool_scale[j])
        else:
            y_lat, y_ctx = attention_mixer(h_lat, h_ctx, w_qkv[j], g_q[j], g_k[j], w_o[j], not last)
        x = x + gt1 * rms_norm(y_lat, g_mix_post[i])
        m_lat = sq_relu_mlp(modulate(rms_norm(x, g_mlp_pre[i]), sh2, sc2), w_mlp_in[i], w_mlp_out[i])
        x = x + gt2 * rms_norm(m_lat, g_mlp_post[i])
        if not last:
            ctx = ctx + cgt1 * rms_norm(y_ctx, g_mix_post[i])
            m_ctx = sq_relu_mlp(modulate(rms_norm(ctx, g_mlp_pre[i]), csh2, csc2), w_mlp_in[i], w_mlp_out[i])
            ctx = ctx + cgt2 * rms_norm(m_ctx, g_mlp_post[i])
    return x
```

```python
import contextlib
import os
import numpy as np
import ml_dtypes
import concourse.bass as bass
import concourse.mybir as mybir
from concourse.bass_utils import run_bass_kernel_spmd

F32 = mybir.dt.float32
BF16 = mybir.dt.bfloat16
AF = mybir.ActivationFunctionType
ALU = mybir.AluOpType
NPBF = ml_dtypes.bfloat16

ENGS = ("pe", "act", "dve", "pool", "sp")
T = 2048
TC = 64
HO = 8
EPS = 1e-6
NWB = 8


class Op:
    __slots__ = ("eng", "emit", "deps", "dma", "sig", "cnt", "sem", "idx", "nosig")

    def __init__(self, eng, emit, dma):
        self.eng, self.emit, self.dma = eng, emit, dma
        self.nosig = False
        self.deps, self.sig, self.cnt, self.sem = [], False, 0, None


class Sched:
    def __init__(self, n_dma_sems=8):
        self.ops = {e: [] for e in ENGS}
        self.all_ops = []
        self.last_w, self.readers, self.alias_deps = {}, {}, {}
        self.n_dma_sems = n_dma_sems
        self.dma_ring = {e: [] for e in ENGS}

    def alias(self, new_buf, old_bufs):
        s = self.alias_deps.setdefault(new_buf, [])
        olds, seen = set(old_bufs), set(id(o) for o in s)
        for k, w in self.last_w.items():
            if k[0] in olds and id(w) not in seen:
                s.append(w); seen.add(id(w))
        for k, rs in self.readers.items():
            if k[0] in olds:
                for r in rs:
                    if id(r) not in seen:
                        s.append(r); seen.add(id(r))

    def add(self, eng, emit, reads=(), writes=(), dma=False, nosig=False, after=()):
        op = Op(eng, emit, dma)
        op.nosig = nosig
        deps = {id(o): o for o in after}
        for k in reads:
            w = self.last_w.get(k)
            if w is not None:
                deps[id(w)] = w
            self.readers.setdefault(k, []).append(op)
            for o in self.alias_deps.get(k[0], ()):
                deps[id(o)] = o
        for k in writes:
            w = self.last_w.get(k)
            if w is not None:
                deps[id(w)] = w
            for r in self.readers.get(k, ()):
                deps[id(r)] = r
            self.readers[k] = []
            self.last_w[k] = op
            for o in self.alias_deps.get(k[0], ()):
                deps[id(o)] = o
        deps.pop(id(op), None)
        if dma:
            ring = self.dma_ring[eng]
            if len(ring) >= self.n_dma_sems:
                prev = ring[len(ring) - self.n_dma_sems]
                deps[id(prev)] = prev
            ring.append(op)
        best, keep = {}, []
        for d in deps.values():
            if d.dma:
                keep.append(d)
            elif d.eng not in best or d.idx > best[d.eng].idx:
                best[d.eng] = d
        op.deps = keep + list(best.values())
        op.idx = len(self.all_ops)
        self.ops[eng].append(op)
        self.all_ops.append(op)
        return op

    def emit_all(self, nc, st, final_waits=()):
        SAME_OK = ("pe", "act", "dve") if os.environ.get("KSAME", "0") == "1" else ("pe",)

        def skip(d, op):
            return d.eng == op.eng and d.eng in SAME_OK and not d.dma and not op.dma
        for op in self.all_ops:
            for d in op.deps:
                if not skip(d, op):
                    d.sig = True
        for op in final_waits:
            op.sig = True
        for op in self.all_ops:
            assert not (op.sig and op.nosig), "fp32 matmul must not carry a semaphore increment"
        esem = {e: st.enter_context(nc.semaphore(f"s_{e}")) for e in ENGS}
        dsem = {e: [st.enter_context(nc.semaphore(f"d_{e}{i}")) for i in range(self.n_dma_sems)]
                for e in ENGS if self.dma_ring[e]}
        for e in ENGS:
            k, ecnt, dcnt = 0, 0, [0] * self.n_dma_sems
            for op in self.ops[e]:
                if op.dma:
                    i = k % self.n_dma_sems
                    k += 1
                    dcnt[i] += 16
                    op.sem, op.cnt, op.sig = dsem[e][i], dcnt[i], True
                elif op.sig:
                    ecnt += 1
                    op.sem, op.cnt = esem[e], ecnt
        block = st.enter_context(nc.Block())
        engmap = {"pe": block.tensor, "act": block.scalar, "dve": block.vector,
                  "pool": block.gpsimd, "sp": block.sync}

        def make(e):
            def body(eng):
                waited = {}
                for op in self.ops[e]:
                    need = {}
                    for d in op.deps:
                        if skip(d, op):
                            continue
                        key = id(d.sem)
                        if d.cnt > waited.get(key, 0) and d.cnt > need.get(key, (0, None))[0]:
                            need[key] = (d.cnt, d.sem)
                    for key, (cnt, sem) in need.items():
                        eng.wait_ge(sem, cnt)
                        waited[key] = cnt
                    ins = op.emit(eng)
                    if op.sig:
                        ins.then_inc(op.sem, 16 if op.dma else 1)
                if e == "sp":
                    for op in final_waits:
                        eng.wait_ge(op.sem, op.cnt)
            return body

        for e in ENGS:
            if self.ops[e] or e == "sp":
                engmap[e](make(e))


class Prog:
    def __init__(self, mode):
        self.mode = mode
        self.nc = bass.Bass("TRN2", target_bir_lowering=False)
        self.S = Sched(int(os.environ.get("KSEMS", "8")))
        self.st = contextlib.ExitStack()
        self.finals = []
        self.rr = {}

    def din(self, name, shape, dt=F32):
        return self.nc.dram_tensor(name, list(shape), dt, kind="ExternalInput").ap()

    def dout(self, name, shape, dt=F32):
        return self.nc.dram_tensor(name, list(shape), dt, kind="ExternalOutput").ap()

    def sb(self, name, shape, dt=F32):
        return self.st.enter_context(self.nc.sbuf_tensor("sb_" + name, list(shape), dt))

    def ps(self, name):
        return self.st.enter_context(self.nc.psum_tensor(name, [128, 512], F32))

    def nxt(self, name, n):
        i = self.rr.get(name, 0)
        self.rr[name] = i + 1
        return i % n

    def ew(self, name):
        return ("dve", "pool")[self.nxt(name, 2)]

    def load(self, dst, src, wkey, eng="sp", after=()):
        return self.S.add(eng, lambda e: e.dma_start(out=dst, in_=src), writes=[wkey], dma=True, after=after)

    def dint(self, name, shape, dt=F32):
        return self.nc.dram_tensor(name, list(shape), dt, kind="Internal").ap()

    def store(self, dst, src, rkey, eng="sp"):
        op = self.S.add(eng, lambda e: e.dma_start(out=dst, in_=src), reads=[rkey], dma=True)
        self.finals.append(op)
        return op


class _Cut(Exception):
    pass


def _build(mode):
    P = Prog(mode)
    CUT = int(os.environ.get("KCUT", "99"))
    nc, S = P.nc, P.S
    A = mode in ("A", "F")
    Bm = mode in ("B", "F")
    FU = mode == "F"
    DBG = os.environ.get("KDBG") == "1" and not FU
    d_cond = P.din("cond", [128, 16])
    d_wada = P.din("wada", [48, 128, 1024])
    d_bada = P.din("bada", [128, 48])
    d_gains = P.din("gains", [128, 32])
    d_win = P.din("w_in", [32, 128, 1024])
    d_wout = P.din("w_out", [32, 128, 1024])
    if A:
        d_x = P.din("xT", [1024, T + 2 * HO])
        d_ctx = P.din("ctxT", [1024, TC + 2 * HO])
        if not FU:
            d_wada1 = P.din("wada1", [48, 128, 1024])
            o_m1 = P.dout("mods1", [128, 96])
        d_bada1 = P.din("bada1", [128, 48])
        d_g1pre = P.din("g1pre", [128, 8])
        d_wpool = P.din("w_pool", [2, 128, 1024])
        d_pscale = P.din("pscale", [128, 8])
        d_edge = P.din("edge", [128, 64])
        d_hmask = P.din("hmask", [128, 2])
        d_wqkv = P.din("w_qkv", [12, 128, 1024])
        d_gqk = P.din("gqk", [128, 2])
        d_cos = P.din("cosT", [128, T])
        d_sin = P.din("sinT", [128, T])
        d_rot = P.din("rot", [128, 128])
        if FU:
            o_q = P.dint("q_d", [8, 128, T], BF16)
            k_loc = P.dint("k_loc", [256, T + TC], BF16)
            o_k = k_loc.rearrange("(h d) t -> h d t", h=2)
            o_v = P.dint("v_loc", [T + TC, 256], BF16)
            k_all = P.dint("k_all", [4 * 256, T + TC], BF16)
            v_all = P.dint("v_all", [4 * (T + TC), 256], BF16)
        else:
            o_x1 = P.dout("x1T", [1024, T])
            o_q = P.dout("qT", [8, 128, T], BF16)
            o_k = P.dout("kT", [2, 128, T + TC], BF16)
            o_v = P.dout("v", [T + TC, 256], BF16)
        if DBG:
            g_h2 = P.dout("dbg_h2", [128, 8, 512], BF16)
            g_u = P.dout("dbg_u", [128, 32, 512], BF16)
            g_y = P.dout("dbg_ysb", [128, 8, 512])
            g_w = P.dout("dbg_wb", [128, 1024], BF16)
            g_m = P.dout("dbg_mods", [128, 96])
            g_G = P.dout("dbg_G", [128, 64])
            g_b = P.dout("dbg_bada", [128, 48])
    if Bm:
        if FU:
            d_q = o_q
            d_wadaB = P.din("wadaB", [48, 128, 1024])
            d_gainsB = P.din("gainsB", [128, 32])
            d_winB = P.din("w_inB", [32, 128, 1024])
            d_woutB = P.din("w_outB", [32, 128, 1024])
        else:
            d_x = P.din("x1T", [1024, T])
            d_q = P.din("qT", [8, 128, T], BF16)
            d_k = P.din("kTall", [2, 128, 8448], BF16)
            d_v = P.din("vall", [2, 8448, 128], BF16)
            d_wadaB, d_winB, d_woutB = d_wada, d_win, d_wout
            d_mods1 = P.din("mods1", [128, 96])
        d_wo = P.din("w_o", [8, 128, 1024])
        d_gqkrow = P.din("gqkrow", [1, 256])
        o_y = P.dout("yT", [1024, T])
        if DBG:
            g_ot = P.dout("dbg_ot", [128, 8, 512], BF16)
            g_yw = P.dout("dbg_yw", [128, 8, 512])
            g_xm = P.dout("dbg_xm", [128, 8, 512])
            g_m = P.dout("dbg_mods", [128, 96])
            g_nc = P.dout("dbg_negc", [128, 1])

    XW = T + 2 * HO
    XOFF = HO if A else 0
    X = P.sb("X", [128, 8, XW if A else T])
    BIG = P.sb("BIG", [128, 17408])
    MID = P.sb("MID", [128, 8704])
    STG = [P.sb(f"stg{i}", [128, 1024]) for i in range(2)]
    WB = [P.sb(f"wb{i}", [128, 1024], BF16) for i in range(NWB)]
    rstd = [P.sb(f"rstd{i}", [128, 512]) for i in range(2)]
    tmp = [P.sb(f"tmp{i}", [128, 512]) for i in range(2)]
    sq = [P.sb(f"sq{i}", [128, 512], BF16) for i in range(2)]
    ones = P.sb("ones", [128, 128], BF16)
    cond = P.sb("cond", [128, 8, 2])
    condb = P.sb("condb", [128, 8, 2], BF16)
    mods = P.sb("mods", [128, 48, 2])
    bada = P.sb("bada", [128, 48])
    gains = P.sb("gains", [128, 4, 8])
    G1 = P.sb("G1", [128, 8, 2]); Gp1 = P.sb("Gp1", [128, 8, 2])
    G2 = P.sb("G2", [128, 8, 2]); Gp2 = P.sb("Gp2", [128, 8, 2])
    PSALL = P.st.enter_context(nc.psum_tensor("psall", [128, 4096], F32))
    psb = [PSALL[:, i * 512:(i + 1) * 512] for i in range(8)]
    u = BIG[:].bitcast(BF16).rearrange("p (c t) -> p c t", c=32)
    h2 = MID[:].bitcast(BF16)[:, 0:8 * 1088].rearrange("p (c t) -> p c t", c=8)
    ysb = MID[:, 0:8 * 1088].rearrange("p (c t) -> p c t", c=8)
    if A:
        XC = P.sb("XC", [128, 8, TC + 2 * HO])
        pscale = P.sb("pscale", [128, 8])
        edge = P.sb("edge", [128, 4, 16])
        hmask = P.sb("hmask", [128, 2])
        gqk = P.sb("gqk", [128, 2])
        mods1 = P.sb("mods1", [128, 48, 2])
        bada1 = P.sb("bada1", [128, 48])
        g1pre = P.sb("g1pre", [128, 8])
        G1b = P.sb("G1b", [128, 8, 2])
        hbuf = BIG[:, 0:8 * XW].rearrange("p (c t) -> p c t", c=8)
        diff = MID[:].bitcast(BF16)[:, 0:2048].rearrange("p (c t) -> p c t", c=8)
        ym = MID[:, 1024:1024 + 2048].rearrange("p (c t) -> p c t", c=8)
        pscr = [MID[:, 3072 + i * 272: 3072 + (i + 1) * 272] for i in range(4)]
        wpool = MID[:, 4160:4160 + 1024].bitcast(BF16)
        HC = BIG[:, 8 * XW: 8 * XW + 8 * (TC + 2 * HO)].rearrange("p (c t) -> p c t", c=8)
        ostg = [BIG[:, 2 * T + 1024 + i * 256: 2 * T + 1024 + (i + 1) * 256].bitcast(BF16) for i in range(2)]
        rot = BIG[:, 2 * T + 1536: 2 * T + 1536 + 128]
    if Bm:
        gainsB = P.sb("gainsB", [128, 4, 8]) if FU else gains
        gqkrow = P.sb("gqkrow", [1, 256])
        negc = P.sb("negc", [128, 1])
        c1 = P.sb("c1", [1, 4])
        ones32 = P.sb("ones32", [1, 128])
        BB = BIG[:].bitcast(BF16)
        KT = BB[:, 0:8448]
        V = BB[:, 8448:2 * 8448].rearrange("p (k d) -> p k d", k=66)
        OTall = BB[:, 2 * 8448:2 * 8448 + 8 * T].rearrange("p (h t) -> p h t", h=8)
        MB = MID[:].bitcast(BF16)
        QTg = [MB[:, i * 2048:(i + 1) * 2048].rearrange("p (h t) -> p h t", h=4) for i in range(2)]
        yw = MID[:, 0:4096].rearrange("p (c t) -> p c t", c=8)
        Pb = [MB[:, 4096 + i * 1024: 4096 + (i + 1) * 1024] for i in range(3)]
        Pz = [MB[:, 7168 + i * 512: 7168 + (i + 1) * 512] for i in range(3)]

    xv = d_x.rearrange("(c p) t -> p c t", p=128)
    for kc in range(8):
        P.load(X[:, kc, 0:(XW if A else T)], xv[:, kc, :], ("X", kc))
    if A:
        cv = d_ctx.rearrange("(c p) t -> p c t", p=128)
        P.load(XC[:], cv, ("XC",))
    S.add("pool", lambda e: e.memset(ones[:], 1.0), writes=[("ones",)])
    P.load(cond[:].rearrange("p c v -> p (c v)"), d_cond, ("cond",))
    P.load(bada[:], d_bada, ("bada",))
    P.load(gains[:].rearrange("p a c -> p (a c)"), d_gains, ("gains",))
    S.add("act", lambda e: e.activation(out=condb[:], in_=cond[:], func=AF.Silu),
          reads=[("cond",)], writes=[("condb",)])

    def stage(src):
        i = P.nxt("stg", 2)
        P.load(STG[i][:], src, ("stg", i))
        return i

    def cast(si, ring=None):
        i = P.nxt("wb", NWB) if ring is None else ring[0] + P.nxt(("wb", ring), ring[1])
        eng = ("act", "dve")[P.nxt("casteng", 2)]
        if eng == "act":
            S.add("act", lambda e: e.copy(out=WB[i][:], in_=STG[si][:]), reads=[("stg", si)], writes=[("wb", i)])
        else:
            S.add("dve", lambda e: e.tensor_copy(out=WB[i][:], in_=STG[si][:]), reads=[("stg", si)], writes=[("wb", i)])
        return i

    def wstream(chunks, depth, ring=None):
        st_ = {"next": 0, "wi": {}}

        def get(i):
            while st_["next"] <= min(i + depth, len(chunks) - 1):
                k = st_["next"]
                st_["wi"][k] = cast(stage(chunks[k]), ring)
                st_["next"] += 1
            return st_["wi"][i]
        return get

    PS_MODS = 7

    def pe_marker(reads, writes):
        S.add("pe", lambda e: e.matmul(psb[PS_MODS][:, 510:512], lhsT=ones[:], rhs=ones[:, 0:2], start=True, stop=True),
              reads=list(reads) + [("ones",)], writes=list(writes) + [("ps_mark",)])

    def ada_mm(dw, chunks, oc0, ring=None):
        pm = psb[PS_MODS]
        wget = wstream([dw[ci] for ci in chunks], 3, ring)
        for idx, ci in enumerate(chunks):
            wi = wget(idx)
            oc = ci - oc0
            for kc in range(8):
                S.add("pe", lambda e, wi=wi, kc=kc, oc=oc: e.matmul(
                    pm[:, 2 * oc:2 * oc + 2], lhsT=WB[wi][:, kc * 128:(kc + 1) * 128],
                    rhs=condb[:, kc, :], start=(kc == 0), stop=(kc == 7)),
                    reads=[("wb", wi), ("condb",)], writes=[("ps", PS_MODS)])

    def ada_evac(lo, hi, mods_t, mkey, bada_t, bkey, oc0):
        pm = psb[PS_MODS]
        a, b_ = lo - oc0, hi - oc0
        for v in range(2):
            S.add("dve", lambda e, v=v: e.tensor_tensor(
                out=mods_t[:, a:b_, v], in0=pm[:, 2 * a:2 * b_].rearrange("p (o v) -> p o v", v=2)[:, :, v],
                in1=bada_t[:, lo:hi], op=ALU.add),
                reads=[("ps", PS_MODS), bkey], writes=[mkey])

    def ada(dw, chunks, mods_t, mkey, bada_t, bkey, oc0):
        ada_mm(dw, chunks, oc0)
        ada_evac(chunks[0], chunks[-1] + 1, mods_t, mkey, bada_t, bkey, oc0)

    def combine(out_t, okey, gain_ap, gkey, mods_t, mkey, mi, plus_one):
        for v in range(2):
            if plus_one:
                S.add("dve", lambda e, v=v: e.scalar_tensor_tensor(
                    out=out_t[:, :, v], in0=mods_t[:, mi * 8:(mi + 1) * 8, v], scalar=1.0, in1=gain_ap,
                    op0=ALU.add, op1=ALU.mult), reads=[mkey, gkey], writes=[okey])
            else:
                S.add("dve", lambda e, v=v: e.tensor_tensor(
                    out=out_t[:, :, v], in0=mods_t[:, mi * 8:(mi + 1) * 8, v], in1=gain_ap, op=ALU.mult),
                    reads=[mkey, gkey], writes=[okey])

    PS_STAT = 6

    def rms_rstd(src_fn, src_keys, n, scale_ap=None):
        pst = psb[PS_STAT]
        for kc in range(8):
            qi = P.nxt("sq", 2)
            if kc % 2 == 0:
                S.add("act", lambda e, kc=kc, qi=qi: e.activation(out=sq[qi][:, :n], in_=src_fn(kc), func=AF.Square),
                      reads=[src_keys(kc)], writes=[("sq", qi)])
            else:
                S.add("dve", lambda e, kc=kc, qi=qi: e.tensor_tensor(out=sq[qi][:, :n], in0=src_fn(kc), in1=src_fn(kc), op=ALU.mult),
                      reads=[src_keys(kc)], writes=[("sq", qi)])
            S.add("pe", lambda e, kc=kc, qi=qi: e.matmul(pst[:, :n], lhsT=ones[:], rhs=sq[qi][:, :n],
                                                         start=(kc == 0), stop=(kc == 7)),
                  reads=[("sq", qi), ("ones",)], writes=[("ps", PS_STAT)])
        r = P.nxt("rstd", 2)
        S.add("act", lambda e: e.activation(out=rstd[r][:, :n], in_=pst[:, :n], func=AF.Ln, bias=EPS, scale=1.0 / 1024),
              reads=[("ps", PS_STAT)], writes=[("rstd", r)])
        S.add("act", lambda e: e.activation(out=rstd[r][:, :n], in_=rstd[r][:, :n], func=AF.Exp, scale=-0.5),
              reads=[("rstd", r)], writes=[("rstd", r)])
        return r

    def modulate(src_fn, src_keys, n, r, Gt, gk, sht, shk, shi, v, dst_fn, dst_keys):
        for kc in range(8):
            ti = P.nxt("tmp", 2)
            S.add("dve", lambda e, kc=kc, ti=ti: e.tensor_tensor(out=tmp[ti][:, :n], in0=src_fn(kc), in1=rstd[r][:, :n], op=ALU.mult),
                  reads=[src_keys(kc), ("rstd", r)], writes=[("tmp", ti)])
            S.add("act", lambda e, kc=kc, ti=ti: e.activation(out=dst_fn(kc), in_=tmp[ti][:, :n], func=AF.Identity,
                                                              bias=sht[:, shi * 8 + kc, v:v + 1], scale=Gt[:, kc, v:v + 1]),
                  reads=[("tmp", ti), gk, shk], writes=[dst_keys(kc)])

    def resid_add(xfn, xkeys, yfn, ykeys, n, r, Gpt, gpk, v):
        for kc in range(8):
            ti = P.nxt("tmp", 2)
            S.add("dve", lambda e, kc=kc, ti=ti: e.scalar_tensor_tensor(
                out=tmp[ti][:, :n], in0=yfn(kc), scalar=Gpt[:, kc, v:v + 1], in1=rstd[r][:, :n], op0=ALU.mult, op1=ALU.mult),
                reads=[ykeys(kc), gpk, ("rstd", r)], writes=[("tmp", ti)])
            S.add("pool", lambda e, kc=kc, ti=ti: e.tensor_tensor(out=xfn(kc), in0=xfn(kc), in1=tmp[ti][:, :n], op=ALU.add),
                  reads=[("tmp", ti), xkeys(kc)], writes=[xkeys(kc)])

    dbg_state = {"n": 0}

    def mlp(groups, d_win=d_win, d_wout=d_wout):
        dbg = A and DBG and dbg_state["n"] == 0
        dbg_state["n"] += 1
        S.alias("h2", ["ysb", "diff", "ym", "pscr", "QT", "Pb", "Pz", "wpool", "yw"])
        S.alias("u", ["hbuf", "HC", "KV", "OTall", "rope"])
        for (xfn, xkey, n, v, uc) in groups:
            r = rms_rstd(xfn, xkey, n)
            modulate(xfn, xkey, n, r, G2, ("G2",), mods, ("mods",), 3, v,
                     lambda kc, uc=uc, n=n: h2[:, kc, uc:uc + n], lambda kc, uc=uc: ("h2", kc, uc))
        if dbg:
            P.store(g_m, mods[:].rearrange("p o v -> p (o v)"), ("mods",))
            P.store(g_b, bada[:], ("bada",))
            for i_, (Gt_, gk_) in enumerate(((G1, ("G1",)), (Gp1, ("Gp1",)), (G2, ("G2",)), (Gp2, ("Gp2",)))):
                P.store(g_G[:, i_ * 16:(i_ + 1) * 16], Gt_[:].rearrange("p c v -> p (c v)"), gk_)
            for kc in range(8):
                P.store(g_h2[:, kc, :], h2[:, kc, 0:512], ("h2", kc, 0))
        wget = wstream([d_win[i] for i in range(32)] + [d_wout[i] for i in range(32)], 4)
        for oc in range(32):
            wi = wget(oc)
            if dbg and oc == 5:
                P.store(g_w, WB[wi][:], ("wb", wi))
            for (xfn, xkey, n, v, uc) in groups:
                b = P.nxt("psmain", 4)
                for kc in range(8):
                    S.add("pe", lambda e, b=b, kc=kc, wi=wi, uc=uc, n=n: e.matmul(
                        psb[b][:, :n], lhsT=WB[wi][:, kc * 128:(kc + 1) * 128],
                        rhs=h2[:, kc, uc:uc + n], start=(kc == 0), stop=(kc == 7)),
                        reads=[("wb", wi), ("h2", kc, uc)], writes=[("ps", b)])
                ti = P.nxt("tmp", 2)
                S.add("act", lambda e, b=b, ti=ti, n=n: e.activation(out=tmp[ti][:, :n], in_=psb[b][:, :n], func=AF.Relu),
                      reads=[("ps", b)], writes=[("tmp", ti)])
                S.add("dve", lambda e, ti=ti, oc=oc, uc=uc, n=n: e.tensor_tensor(
                    out=u[:, oc, uc:uc + n], in0=tmp[ti][:, :n], in1=tmp[ti][:, :n], op=ALU.mult),
                    reads=[("tmp", ti)], writes=[("u", oc, uc)])
        if dbg:
            for kc in range(32):
                P.store(g_u[:, kc, :], u[:, kc, 0:512], ("u", kc, 0))
        S.alias("ysb", ["h2"])
        for oc in range(8):
            wis = [wget(32 + 4 * oc + q) for q in range(4)]
            for (xfn, xkey, n, v, uc) in groups:
                b = P.nxt("psmain", 4)
                for kc in range(32):
                    wi = wis[kc // 8]
                    S.add("pe", lambda e, b=b, kc=kc, wi=wi, uc=uc, n=n: e.matmul(
                        psb[b][:, :n], lhsT=WB[wi][:, (kc % 8) * 128:(kc % 8 + 1) * 128],
                        rhs=u[:, kc, uc:uc + n], start=(kc == 0), stop=(kc == 31)),
                        reads=[("wb", wi), ("u", kc, uc)], writes=[("ps", b)])
                S.add("dve", lambda e, b=b, oc=oc, uc=uc, n=n: e.tensor_copy(out=ysb[:, oc, uc:uc + n], in_=psb[b][:, :n]),
                      reads=[("ps", b)], writes=[("ysb", oc, uc)])
        if dbg:
            for kc in range(8):
                P.store(g_y[:, kc, :], ysb[:, kc, 0:512], ("ysb", kc, 0))
        for (xfn, xkey, n, v, uc) in groups:
            yfn = lambda kc, uc=uc, n=n: ysb[:, kc, uc:uc + n]
            ykey = lambda kc, uc=uc: ("ysb", kc, uc)
            r = rms_rstd(yfn, ykey, n)
            resid_add(xfn, xkey, yfn, ykey, n, r, Gp2, ("Gp2",), v)

    def xmain(g):
        off = HO if A else 0
        return (lambda kc: X[:, kc, off + g * 512: off + (g + 1) * 512]), (lambda kc: ("X", kc, off + g * 512))

    S.alias_deps["X"] = [S.last_w[("X", kc)] for kc in range(8)]
    if A:
        S.alias_deps["XC"] = []

    if A:
        P.load(pscale[:], d_pscale, ("pscale",))
        P.load(edge[:].rearrange("p a c -> p (a c)"), d_edge, ("edge",))
        P.load(hmask[:], d_hmask, ("hmask",))
        P.load(gqk[:], d_gqk, ("gqk",))
        P.load(bada1[:], d_bada1, ("bada1",))
        P.load(g1pre[:], d_g1pre, ("g1pre",))
        for i in range(2):
            wpi = stage(d_wpool[i])
            S.add("dve", lambda e, i=i, wpi=wpi: e.tensor_copy(out=wpool[:, i * 1024:(i + 1) * 1024], in_=STG[wpi][:]),
                  reads=[("stg", wpi)], writes=[("wpool", i)])
        ada(d_wada, list(range(16)), mods, ("mods", 0), bada, ("bada",), 0)
        combine(G1, ("G1",), gains[:, 0, :], ("gains",), mods, ("mods", 0), 1, True)

        def cut(k):
            if CUT <= k and not FU:
                x1v_ = o_x1.rearrange("(c p) t -> p c t", p=128)
                for kc_ in range(8):
                    S.add("sp", lambda e, kc_=kc_: e.dma_start(out=x1v_[:, kc_, :], in_=X[:, kc_, HO:HO + T]),
                          reads=[("X", kc_, HO + g_ * 512) for g_ in range(4)], dma=True)
                    P.finals.append(S.all_ops[-1])
                if DBG:
                    P.store(g_m, mods[:].rearrange("p o v -> p (o v)"), ("mods",))
                    S.add("dve", lambda e: e.tensor_copy(out=tmp[0][:, 0:96], in_=psb[PS_MODS][:, 0:96]),
                          reads=[("ps", PS_MODS)], writes=[("tmp", 0)])
                    P.store(g_y[:, 0, 0:96], tmp[0][:, 0:96], ("tmp", 0))
                raise _Cut(P)
        def hgroups(src, skey, dst, dkey, width, v):
            segs = [(0, HO, 0), (width - HO, HO, 1)]
            c0 = HO
            while c0 < width - HO:
                n = min(512, width - HO - c0)
                segs.append((c0, n, None))
                c0 += n
            for (c0, n, mk) in segs:
                sfn = lambda kc, c0=c0, n=n: src[:, kc, c0:c0 + n]
                sk = lambda kc, c0=c0: (skey, kc, c0)
                dfn = lambda kc, c0=c0, n=n: dst[:, kc, c0:c0 + n]
                dk = lambda kc, c0=c0: (dkey, kc, c0)
                r = rms_rstd(sfn, sk, n)
                modulate(sfn, sk, n, r, G1, ("G1",), mods, ("mods", 0), 0, v, dfn, dk)
                if mk is not None:
                    for kc in range(8):
                        S.add("dve", lambda e, kc=kc, c0=c0, n=n, mk=mk: e.tensor_scalar(
                            out=dst[:, kc, c0:c0 + n], in0=dst[:, kc, c0:c0 + n], scalar1=hmask[:, mk:mk + 1], scalar2=None, op0=ALU.mult),
                            reads=[("hmask",), dk(kc)], writes=[dk(kc)])
        cut(1)
        S.alias_deps["XCs"] = [S.last_w[("XC",)]]
        hgroups(X, "X", hbuf, "hbuf", XW, 0)
        hgroups(XC, "XCs", HC, "HC", TC + 2 * HO, 1)

        def hkeys(dkey, kc, lo, hi, width):
            ks = []
            segs = [(0, HO), (width - HO, HO)]
            c0 = HO
            while c0 < width - HO:
                n = min(512, width - HO - c0)
                segs.append((c0, n)); c0 += n
            for (s0, n) in segs:
                if s0 < hi and s0 + n > lo:
                    ks.append((dkey, kc, s0))
            return ks

        def pool_group(hb, hkey, width, c0, n, v, xsrc, xkey, left_edge, right_edge):
            S.alias("diff", ["h2", "ysb"]); S.alias("ym", ["h2", "ysb"]); S.alias("pscr", ["h2", "ysb"])
            base = c0 - HO
            for g, w in enumerate((2, 4, 8, 16)):
                for kc in (2 * g, 2 * g + 1):
                    hh = lambda a, b, kc=kc: hb[:, kc, base + a: base + b]
                    hk = hkeys(hkey, kc, base, base + n + 16, width)
                    pa, pb_, pc, pS = pscr
                    eng = P.ew("pool_eng")

                    def tt(out, i0, i1, reads, writes, eng=eng):
                        S.add(eng, lambda e: e.tensor_tensor(out=out, in0=i0, in1=i1, op=ALU.add), reads=reads, writes=writes)
                    kA, kB, kC, kS = ("pscr", 0), ("pscr", 1), ("pscr", 2), ("pscr", 3)
                    if w == 2:
                        tt(pS[:, :n], hh(7, 7 + n), hh(8, 8 + n), hk, [kS])
                    else:
                        tt(pa[:, :n + 15], hh(0, n + 15), hh(1, n + 16), hk, [kA])
                        if w == 4:
                            tt(pS[:, :n], pa[:, 6:6 + n], pa[:, 8:8 + n], [kA], [kS])
                        else:
                            tt(pb_[:, :n + 13], pa[:, 0:n + 13], pa[:, 2:n + 15], [kA], [kB])
                            if w == 8:
                                tt(pS[:, :n], pb_[:, 4:4 + n], pb_[:, 8:8 + n], [kB], [kS])
                            else:
                                tt(pc[:, :n + 9], pb_[:, 0:n + 9], pb_[:, 4:n + 13], [kB], [kC])
                                tt(pS[:, :n], pc[:, 0:n], pc[:, 8:8 + n], [kC], [kS])
                    S.add("dve", lambda e, kc=kc, w=w: e.scalar_tensor_tensor(
                        out=diff[:, kc, :n], in0=pS[:, :n], scalar=1.0 / w, in1=hb[:, kc, c0:c0 + n], op0=ALU.mult, op1=ALU.subtract),
                        reads=[kS] + hk, writes=[("diff", kc)])
                    for (flag, cs, ei) in ((left_edge, 0, 0), (right_edge, n - 8, 8)):
                        if flag:
                            S.add("dve", lambda e, cs=cs, ei=ei, g=g: e.tensor_tensor(
                                out=pS[:, cs:cs + 8], in0=pS[:, cs:cs + 8], in1=edge[:, g, ei:ei + 8], op=ALU.mult),
                                reads=[kS, ("edge",)], writes=[kS])
                            S.add("dve", lambda e, cs=cs, kc=kc: e.tensor_tensor(
                                out=diff[:, kc, cs:cs + 8], in0=pS[:, cs:cs + 8], in1=hb[:, kc, c0 + cs:c0 + cs + 8], op=ALU.subtract),
                                reads=[kS] + hk, writes=[("diff", kc)])
                for ol in range(2):
                    oc = 2 * g + ol
                    b = P.nxt("psmain", 4)
                    for kl in range(2):
                        S.add("pe", lambda e, b=b, g=g, kl=kl, ol=ol: e.matmul(
                            psb[b][:, :n], lhsT=wpool[:, g * 512 + kl * 256 + ol * 128: g * 512 + kl * 256 + (ol + 1) * 128],
                            rhs=diff[:, 2 * g + kl, :n], start=(kl == 0), stop=(kl == 1)),
                            reads=[("wpool", g // 2), ("diff", 2 * g + kl)], writes=[("ps", b)])
                    S.add("dve", lambda e, b=b, oc=oc: e.tensor_scalar(
                        out=ym[:, oc, :n], in0=psb[b][:, :n], scalar1=pscale[:, oc:oc + 1], scalar2=None, op0=ALU.mult),
                        reads=[("ps", b), ("pscale",)], writes=[("ym", oc)])
            yfn = lambda kc: ym[:, kc, :n]
            ykey = lambda kc: ("ym", kc)
            r = rms_rstd(yfn, ykey, n)
            resid_add(xsrc, xkey, yfn, ykey, n, r, Gp1, ("Gp1",), v)

        cut(2)
        S.alias("hbuf", ["u"]); S.alias("HC", ["u"])
        ada(d_wada, list(range(16, 24)), mods, ("mods", 1), bada, ("bada",), 0)
        combine(Gp1, ("Gp1",), gains[:, 1, :], ("gains",), mods, ("mods", 1), 2, False)
        for g in range(8):
            c0 = HO + g * 256
            pool_group(hbuf, "hbuf", XW, c0, 256, 0,
                       lambda kc, c0=c0: X[:, kc, c0:c0 + 256], lambda kc, c0=c0: ("X", kc, HO + ((c0 - HO) // 512) * 512), g == 0, g == 7)
            ada_mm(d_wada, list(range(24 + 3 * g, 27 + 3 * g)), 0)
        ada_evac(24, 48, mods, ("mods",), bada, ("bada",), 0)
        combine(G2, ("G2",), gains[:, 2, :], ("gains",), mods, ("mods",), 4, True)
        combine(Gp2, ("Gp2",), gains[:, 3, :], ("gains",), mods, ("mods",), 5, False)
        pool_group(HC, "HC", TC + 2 * HO, HO, TC, 1,
                   lambda kc: XC[:, kc, HO:HO + TC], lambda kc: ("XCs", kc, HO), True, True)

        cut(3)
        def lat_group(g, uc):
            c0 = HO + g * 512
            return (lambda kc, c0=c0: X[:, kc, c0:c0 + 512], lambda kc, c0=c0: ("X", kc, c0), 512, 0, uc)
        ctx_group = (lambda kc: XC[:, kc, HO:HO + TC], lambda kc: ("XCs", kc, HO), TC, 1, 1024)
        mlp([lat_group(0, 0), lat_group(1, 512), ctx_group])
        mlp([lat_group(2, 0), lat_group(3, 512)])

        cut(4)
        if not FU:
            x1v = o_x1.rearrange("(c p) t -> p c t", p=128)
            for kc in range(8):
                P.S.add("sp", lambda e, kc=kc: e.dma_start(out=x1v[:, kc, :], in_=X[:, kc, HO:HO + T]),
                        reads=[("X", kc, HO + g * 512) for g in range(4)], dma=True)
                P.finals.append(S.all_ops[-1])
        qstores, kstores, vstores = [], [], []

        d_w1 = d_wadaB if FU else d_wada1
        ada(d_w1, list(range(16)), mods1, ("mods1",), bada1, ("bada1",), 0)
        combine(G1b, ("G1b",), g1pre[:], ("g1pre",), mods1, ("mods1",), 1, True)
        h1 = MID[:].bitcast(BF16)[:, 0:8 * (T + TC)].rearrange("p (c t) -> p c t", c=8)
        S.alias("h1", ["h2", "ysb", "diff", "ym", "pscr", "wpool"])
        S.alias("rope", ["u", "hbuf"])
        cosT = BIG[:, 0:T]; sinT = BIG[:, T:2 * T]
        qn = [BIG[:, 2 * T + i * 512: 2 * T + (i + 1) * 512] for i in range(2)]
        P.load(rot, d_rot, ("rope", "rot"))
        P.load(cosT, d_cos, ("rope", "cos"))
        P.load(sinT, d_sin, ("rope", "sin"))
        grp1 = [(lambda kc, g=g: X[:, kc, HO + g * 512:HO + (g + 1) * 512], lambda kc, g=g: ("X", kc, HO + g * 512), 512, 0, g * 512) for g in range(4)]
        grp1.append((lambda kc: XC[:, kc, HO:HO + TC], lambda kc: ("XCs", kc, HO), TC, 1, T))
        for (xfn, xkey, n, v, uc) in grp1:
            r = rms_rstd(xfn, xkey, n)
            modulate(xfn, xkey, n, r, G1b, ("G1b",), mods1, ("mods1",), 0, v,
                     lambda kc, uc=uc, n=n: h1[:, kc, uc:uc + n], lambda kc, uc=uc: ("h1", kc, uc))
        cut(5)
        items = []
        for hd in range(10):
            for grp in grp1:
                if hd < 8 and grp[3] == 1:
                    continue
                items.append((hd, grp))
        qkvget = wstream([d_wqkv[i] for i in range(12)], 2, (0, 4))

        def make_item(hd, grp):
            (xfn, xkey, n, v, uc) = grp
            isq = hd < 8
            gcol = 0 if isq else 1
            st_ = {}

            def s0():
                wi = qkvget(hd)
                b = st_["b"] = P.nxt("psmain", 4)
                for kc in range(8):
                    S.add("pe", lambda e, b=b, kc=kc, wi=wi: e.matmul(
                        psb[b][:, :n], lhsT=WB[wi][:, kc * 128:(kc + 1) * 128],
                        rhs=h1[:, kc, uc:uc + n], start=(kc == 0), stop=(kc == 7)),
                        reads=[("wb", wi), ("h1", kc, uc)], writes=[("ps", b)])

            def s1():
                b = st_["b"]
                qi = st_["qi"] = P.nxt("sq", 2)
                S.add("act", lambda e: e.activation(out=sq[qi][:, :n], in_=psb[b][:, :n], func=AF.Square),
                      reads=[("ps", b)], writes=[("sq", qi)])

            def s2():
                qi = st_["qi"]
                pb = st_["pb"] = (4, 5)[P.nxt("qkstat", 2)]
                S.add("pe", lambda e: e.matmul(psb[pb][:, :n], lhsT=ones[:], rhs=sq[qi][:, :n], start=True, stop=True),
                      reads=[("sq", qi), ("ones",)], writes=[("ps", pb)])

            def s3():
                pb = st_["pb"]
                r = st_["r"] = P.nxt("rstd", 2)
                S.add("act", lambda e: e.activation(out=rstd[r][:, :n], in_=psb[pb][:, :n], func=AF.Ln, bias=EPS, scale=1.0 / 128),
                      reads=[("ps", pb)], writes=[("rstd", r)])
                S.add("act", lambda e: e.activation(out=rstd[r][:, :n], in_=rstd[r][:, :n], func=AF.Exp, scale=-0.5),
                      reads=[("rstd", r)], writes=[("rstd", r)])

            def s4():
                b, r = st_["b"], st_["r"]
                qj = st_["qj"] = P.nxt("qn", 2)
                S.add("dve", lambda e: e.scalar_tensor_tensor(
                    out=qn[qj][:, :n], in0=psb[b][:, :n], scalar=gqk[:, gcol:gcol + 1], in1=rstd[r][:, :n], op0=ALU.mult, op1=ALU.mult),
                    reads=[("ps", b), ("gqk",), ("rstd", r)], writes=[("rope", "qn", qj)])

            def s5():
                if v != 0:
                    return
                qj = st_["qj"]
                b2 = st_["b2"] = P.nxt("psmain", 4)
                S.add("pe", lambda e: e.matmul(psb[b2][:, :n], lhsT=rot, rhs=qn[qj][:, :n], start=True, stop=True),
                      reads=[("rope", "rot"), ("rope", "qn", qj)], writes=[("ps", b2)], nosig=True)
                pe_marker([("rope", "rot"), ("rope", "qn", qj)], [("ps", b2)])

            def s6():
                if v != 0:
                    return
                qj, b2 = st_["qj"], st_["b2"]
                ti = st_["ti"] = P.nxt("tmp", 2)
                S.add("dve", lambda e: e.tensor_tensor(out=tmp[ti][:, :n], in0=psb[b2][:, :n], in1=sinT[:, uc:uc + n], op=ALU.mult),
                      reads=[("ps", b2), ("rope", "sin")], writes=[("tmp", ti)])
                S.add("pool", lambda e: e.tensor_tensor(out=qn[qj][:, :n], in0=qn[qj][:, :n], in1=cosT[:, uc:uc + n], op=ALU.mult),
                      reads=[("rope", "qn", qj), ("rope", "cos")], writes=[("rope", "qn", qj)])

            def s7():
                qj = st_["qj"]
                oi = P.nxt("ostg", 2)
                if v == 0:
                    ti = st_["ti"]
                    S.add("dve", lambda e: e.tensor_tensor(out=ostg[oi][:, :n], in0=qn[qj][:, :n], in1=tmp[ti][:, :n], op=ALU.add),
                          reads=[("rope", "qn", qj), ("tmp", ti)], writes=[("rope", "ostg", oi)])
                else:
                    S.add("dve", lambda e: e.tensor_copy(out=ostg[oi][:, :n], in_=qn[qj][:, :n]),
                          reads=[("rope", "qn", qj)], writes=[("rope", "ostg", oi)])
                dst = o_q[hd, :, uc:uc + n] if isq else o_k[hd - 8, :, uc:uc + n]
                (qstores if isq else kstores).append(P.store(dst, ostg[oi][:, :n], ("rope", "ostg", oi)))
            return [s0, s1, s2, s3, s4, s5, s6, s7]

        WAVE = 2
        m1_next = 16
        for i0 in range(0, len(items), WAVE):
            batch = [make_item(*it) for it in items[i0:i0 + WAVE]]
            for si in range(8):
                for stg_ in batch:
                    stg_[si]()
            if not FU and m1_next < 48:
                ada_mm(d_w1, list(range(m1_next, m1_next + 2)), 0, (4, 4))
                m1_next += 2
        if not FU:
            ada_evac(16, 48, mods1, ("mods1b",), bada1, ("bada1",), 0)
            P.store(o_m1[:, 32:96], mods1[:, 16:48, :].rearrange("p o v -> p (o v)"), ("mods1b",))
            P.store(o_m1[:, 0:32], mods1[:, 0:16, :].rearrange("p o v -> p (o v)"), ("mods1",))
        wv = [qkvget(10 + i) for i in range(2)]
        tiles = [(t0, 128) for t0 in range(0, T, 128)] + [(T, TC)]
        for (t0, m) in tiles:
            b = P.nxt("psmain", 4)
            for i in range(2):
                for kc in range(8):
                    S.add("pe", lambda e, b=b, kc=kc, t0=t0, m=m, i=i: e.matmul(
                        psb[b][:m, i * 128:(i + 1) * 128], lhsT=h1[:, kc, t0:t0 + m], rhs=WB[wv[i]][:, kc * 128:(kc + 1) * 128],
                        start=(kc == 0), stop=(kc == 7)),
                        reads=[("wb", wv[i])] + [("h1", kc, (t0 // 512) * 512)], writes=[("ps", b)])
            oi = P.nxt("ostg", 2)
            S.add("dve", lambda e, b=b, oi=oi, m=m: e.tensor_copy(out=ostg[oi][:m, :256], in_=psb[b][:m, :256]),
                  reads=[("ps", b)], writes=[("rope", "ostg", oi)])
            vstores.append(P.store(o_v[t0:t0 + m, :], ostg[oi][:m, :256], ("rope", "ostg", oi)))
        if FU:
            RG = [[0, 1, 2, 3], [4, 5, 6, 7]]
            cck = S.add("pool", lambda e: e.collective_compute("AllGather", ALU.bypass, replica_groups=RG, ins=[k_loc], outs=[k_all]),
                        writes=[("kall",)], dma=True, after=kstores)
            ccv = S.add("pool", lambda e: e.collective_compute("AllGather", ALU.bypass, replica_groups=RG, ins=[o_v], outs=[v_all]),
                        writes=[("vall",)], dma=True, after=vstores)

    if Bm:
        if FU:
            oldB = ["u", "hbuf", "HC", "rope"]
            S.alias("KV", oldB); S.alias("OTall", oldB)
            S.alias("QT", ["h1", "h2", "ysb", "diff", "ym", "pscr", "wpool"])
            P.load(gainsB[:].rearrange("p a c -> p (a c)"), d_gainsB, ("gainsB",))
        P.load(gqkrow[:], d_gqkrow, ("gqkrow",))
        S.add("dve", lambda e: e.tensor_reduce(out=c1[:, 0:2], in_=gqkrow[:].rearrange("p (a d) -> p a d", a=2),
                                                axis=mybir.AxisListType.X, op=ALU.max, apply_absolute_value=True),
              reads=[("gqkrow",)], writes=[("c1",)])
        S.add("dve", lambda e: e.scalar_tensor_tensor(out=c1[:, 2:3], in0=c1[:, 0:1], scalar=-(128.0 ** 0.5), in1=c1[:, 1:2], op0=ALU.mult, op1=ALU.mult),
              reads=[("c1",)], writes=[("c1",)])
        S.add("pool", lambda e: e.memset(ones32[:], 1.0), writes=[("ones32",)])
        S.add("pe", lambda e: e.matmul(psb[PS_MODS][:, 100:101], lhsT=ones32[:], rhs=c1[:, 2:3], start=True, stop=True),
              reads=[("ones32",), ("c1",)], writes=[("ps", PS_MODS)], nosig=True)
        pe_marker([("ones32",), ("c1",)], [("ps", PS_MODS)])
        S.add("dve", lambda e: e.tensor_copy(out=negc[:], in_=psb[PS_MODS][:, 100:101]), reads=[("ps", PS_MODS)], writes=[("negc",)])
        gkB = ("gainsB",) if FU else ("gains",)
        if FU:
            ada(d_wadaB, list(range(16, 48)), mods[:, 16:48, :], ("mods",), bada1, ("bada1",), 16)
        else:
            P.load(mods[:].rearrange("p o v -> p (o v)"), d_mods1, ("mods",))
        combine(Gp1, ("Gp1",), gainsB[:, 1, :], gkB, mods, ("mods",), 2, False)
        combine(G2, ("G2",), gainsB[:, 2, :], gkB, mods, ("mods",), 4, True)
        combine(Gp2, ("Gp2",), gainsB[:, 3, :], gkB, mods, ("mods",), 5, False)
        SCALE = 128.0 ** -0.5
        NK = 66
        PS_S = (0, 1, 2)
        for kvh in range(2):
            for part in range(4):
                if FU:
                    P.load(KT[:, part * 2112:(part + 1) * 2112], k_all[part * 256 + kvh * 128: part * 256 + (kvh + 1) * 128, :],
                           ("KV", "k", part), after=[cck])
                else:
                    P.load(KT[:, part * 2112:(part + 1) * 2112], d_k[kvh, :, part * 2112:(part + 1) * 2112], ("KV", "k", part))
            if FU:
                vv = v_all[:, kvh * 128:(kvh + 1) * 128].rearrange("(k p) d -> p k d", p=128)
            else:
                vv = d_v[kvh].rearrange("(k p) d -> p k d", p=128)
            for part in range(6):
                P.load(V[:, part * 11:(part + 1) * 11, :], vv[:, part * 11:(part + 1) * 11, :], ("KV", "v", part),
                       after=([ccv] if FU else []))
            for qg in range(4):
                qb = P.nxt("QTg", 2)
                for hl in range(4):
                    P.load(QTg[qb][:, hl, :], d_q[4 * kvh + hl, :, qg * 512:(qg + 1) * 512], ("QT", qb, hl),
                           after=(qstores if FU else []))
                for hl in range(4):
                    hh = 4 * kvh + hl
                    bo, bz = 4 + (hl % 2), 6 + (hl % 2)
                    NP_ = NK // 2

                    def s_pair(j, hl=hl, qb=qb):
                        sp_ = j % 2
                        pb_ = j % 3
                        for t_ in range(2):
                            kt = 2 * j + t_
                            part = (kt * 128) // 2112
                            part2 = (kt * 128 + 127) // 2112
                            S.add("pe", lambda e, kt=kt, t_=t_: e.matmul(psb[2 * sp_ + t_], lhsT=KT[:, kt * 128:(kt + 1) * 128],
                                                                       rhs=QTg[qb][:, hl, :], start=True, stop=True),
                                  reads=[("KV", "k", part), ("KV", "k", part2), ("QT", qb, hl)], writes=[("ps", 2 * sp_ + t_)])
                        S.add("act", lambda e: e.activation(out=Pb[pb_], in_=PSALL[:, sp_ * 1024:(sp_ + 1) * 1024], func=AF.Exp,
                                                            bias=negc[:, 0:1], scale=SCALE),
                              reads=[("ps", 2 * sp_), ("ps", 2 * sp_ + 1), ("negc",)], writes=[("Pb", pb_)])
                        S.add("dve", lambda e: e.tensor_tensor(out=Pz[pb_], in0=Pb[pb_][:, 0:512], in1=Pb[pb_][:, 512:1024], op=ALU.add),
                              reads=[("Pb", pb_)], writes=[("Pz", pb_)])

                    def pv_pair(j, bo=bo, bz=bz):
                        sp_ = j % 2
                        pb_ = j % 3
                        for t_ in range(2):
                            kt = 2 * j + t_
                            S.add("pe", lambda e, kt=kt, t_=t_: e.matmul(psb[bo], lhsT=V[:, kt, :], rhs=Pb[pb_][:, t_ * 512:(t_ + 1) * 512],
                                                                       start=(kt == 0), stop=(kt == NK - 1)),
                                  reads=[("KV", "v", kt // 11), ("Pb", pb_)], writes=[("ps", bo)])
                        S.add("pe", lambda e: e.matmul(psb[bz], lhsT=ones[:], rhs=Pz[pb_], start=(j == 0), stop=(j == NP_ - 1)),
                              reads=[("ones",), ("Pz", pb_)], writes=[("ps", bz)])
                    s_pair(0)
                    s_pair(1)
                    for j in range(NP_):
                        if j + 2 < NP_:
                            s_pair(j + 2)
                        pv_pair(j)
                    ti = P.nxt("tmp", 2)
                    S.add("dve", lambda e, ti=ti, bz=bz: e.reciprocal(out=tmp[ti][:], in_=psb[bz]), reads=[("ps", bz)], writes=[("tmp", ti)])
                    S.add("dve", lambda e, ti=ti, bo=bo, hh=hh, qg=qg: e.tensor_tensor(out=OTall[:, hh, qg * 512:(qg + 1) * 512], in0=psb[bo], in1=tmp[ti][:], op=ALU.mult),
                          reads=[("ps", bo), ("tmp", ti)], writes=[("OTall", hh, qg)])
        S.alias("yw", ["QT", "Pb", "Pz", "h1", "h2", "ysb", "diff", "ym", "pscr", "wpool"])
        woget = wstream([d_wo[oc] for qg in range(4) for oc in range(8)], 3)
        for qg in range(4):
            for oc in range(8):
                wi = woget(qg * 8 + oc)
                b = P.nxt("psmain", 4)
                for hh in range(8):
                    S.add("pe", lambda e, b=b, wi=wi, hh=hh, qg=qg: e.matmul(
                        psb[b], lhsT=WB[wi][:, hh * 128:(hh + 1) * 128],
                        rhs=OTall[:, hh, qg * 512:(qg + 1) * 512], start=(hh == 0), stop=(hh == 7)),
                        reads=[("wb", wi), ("OTall", hh, qg)], writes=[("ps", b)])
                S.add("dve", lambda e, b=b, oc=oc: e.tensor_copy(out=yw[:, oc, :], in_=psb[b]), reads=[("ps", b)], writes=[("yw", oc)])
            xfn, xkey = xmain(qg)
            yfn = lambda kc: yw[:, kc, :]
            ykey = lambda kc: ("yw", kc)
            if DBG and qg == 0:
                for kc in range(8):
                    P.store(g_ot[:, kc, :], OTall[:, kc, 0:512], ("OTall", kc, 0))
                    P.store(g_yw[:, kc, :], yw[:, kc, :], ("yw", kc))
                P.store(g_m, mods[:].rearrange("p o v -> p (o v)"), ("mods",))
                P.store(g_nc, negc[:], ("negc",))
            r = rms_rstd(yfn, ykey, 512)
            resid_add(xfn, xkey, yfn, ykey, 512, r, Gp1, ("Gp1",), 0)
            if DBG and qg == 0:
                for kc in range(8):
                    P.store(g_xm[:, kc, :], X[:, kc, 0:512], ("X", kc, 0))
        def lat_group(g, uc):
            xfn, xkey = xmain(g)
            return (xfn, xkey, 512, 0, uc)
        mlp([lat_group(0, 0), lat_group(1, 512)], d_winB, d_woutB)
        mlp([lat_group(2, 0), lat_group(3, 512)], d_winB, d_woutB)
        yv = o_y.rearrange("(c p) t -> p c t", p=128)
        for kc in range(8):
            S.add("sp", lambda e, kc=kc: e.dma_start(out=yv[:, kc, :], in_=X[:, kc, XOFF:XOFF + T]),
                  reads=[("X", kc, XOFF + g * 512) for g in range(4)], dma=True)
            P.finals.append(S.all_ops[-1])

    return P


def build(mode):
    try:
        P = _build(mode)
    except _Cut as c:
        P = c.args[0]
    P.S.emit_all(P.nc, P.st, final_waits=P.finals)
    P.st.close()
    return P.nc


def chunk_in(W, ncols):
    K, N = W.shape
    a = W.reshape(K // 128, 128, N // ncols, ncols)
    return np.ascontiguousarray(a.transpose(2, 1, 0, 3).reshape(N // ncols, 128, (K // 128) * ncols))


def vec8(v):
    return np.ascontiguousarray(v.reshape(-1, 128).T)


_NC_CACHE = {}
FUSED = os.environ.get("KFUSED", "0") == "1"
_DBG = {}


def get_nc(mode):
    if mode not in _NC_CACHE:
        _NC_CACHE[mode] = build(mode)
    return _NC_CACHE[mode]


def kernel(x, c, ctx, c_ctx, w_ada, b_ada, g_mix_pre, g_mix_post, g_mlp_pre, g_mlp_post,
           w_pool, pool_scale, w_qkv, g_q, g_k, w_o, w_mlp_in, w_mlp_out):
    f = lambda a: np.asarray(a, dtype=np.float32)
    x, c, ctx, c_ctx, w_ada, b_ada = map(f, (x, c, ctx, c_ctx, w_ada, b_ada))
    g_mix_pre, g_mix_post, g_mlp_pre, g_mlp_post = map(f, (g_mix_pre, g_mix_post, g_mlp_pre, g_mlp_post))
    w_pool, pool_scale, w_qkv, g_q, g_k, w_o, w_mlp_in, w_mlp_out = map(f, (w_pool, pool_scale, w_qkv, g_q, g_k, w_o, w_mlp_in, w_mlp_out))
    B, L, D = x.shape
    C = ctx.shape[1]

    def common(layer):
        wout = w_mlp_out[layer]
        a = wout.reshape(4, 8, 128, 8, 128)
        wout_c = np.ascontiguousarray(a.transpose(3, 0, 2, 1, 4).reshape(32, 128, 1024))
        return {
            "wada": chunk_in(w_ada[layer], 128),
            "bada": vec8(b_ada[layer]),
            "gains": np.ascontiguousarray(np.concatenate([vec8(g_mix_pre[layer]), vec8(g_mix_post[layer]),
                                                          vec8(g_mlp_pre[layer]), vec8(g_mlp_post[layer])], axis=1)),
            "w_in": chunk_in(w_mlp_in[layer], 128),
            "w_out": wout_c,
        }
    cmA, cmB = common(0), common(1)
    rot = np.zeros((128, 128), np.float32)
    for m in list(range(0, 32)) + list(range(64, 96)):
        rot[m + 32, m] = -1.0
        rot[m, m + 32] = 1.0
    inv_freq = np.power(np.float32(10000.0), -np.arange(0, 64, 2, dtype=np.float32) / np.float32(64))
    wpool_c = np.ascontiguousarray(w_pool[0].reshape(2, 2, 2, 128, 256).transpose(0, 3, 1, 2, 4).reshape(2, 128, 1024))
    wqkv_c = chunk_in(w_qkv[0], 128)
    wada1_c = chunk_in(w_ada[1], 128)
    mapsA = []
    for core in range(8):
        b, j = core // 4, core % 4
        t0 = j * T
        xp = np.zeros((T + 2 * HO, D), np.float32)
        lo, hi = max(t0 - HO, 0), min(t0 + T + HO, L)
        xp[lo - (t0 - HO):hi - (t0 - HO)] = x[b, lo:hi]
        c0 = j * TC
        cp = np.zeros((TC + 2 * HO, D), np.float32)
        lo, hi = max(c0 - HO, 0), min(c0 + TC + HO, C)
        cp[lo - (c0 - HO):hi - (c0 - HO)] = ctx[b, lo:hi]
        cond = np.stack([vec8(c[b]), vec8(c_ctx)], axis=-1).reshape(128, 16)
        edge = np.zeros((4, 16), np.float32)
        for g, w in enumerate((2, 4, 8, 16)):
            for i in range(8):
                tl = i
                cl = (tl + w - w // 2) - max(tl - w // 2, 0) if j == 0 else w
                tr = 8 - i
                cr = min(w - w // 2, tr) + w // 2 if j == 3 else w
                edge[g, i] = 1.0 / cl
                edge[g, 8 + i] = 1.0 / cr
        hmask = np.array([0.0 if j == 0 else 1.0, 0.0 if j == 3 else 1.0], np.float32)
        tt = np.arange(t0, t0 + T)
        row = (tt // 64).astype(np.float32); col = (tt % 64).astype(np.float32)
        ang = np.concatenate([row[None, :] * inv_freq[:, None]] * 2 + [col[None, :] * inv_freq[:, None]] * 2, axis=0)
        m = dict(cmA)
        m.update({
            "cond": np.ascontiguousarray(cond), "xT": np.ascontiguousarray(xp.T), "ctxT": np.ascontiguousarray(cp.T),
            "wada1": wada1_c, "bada1": vec8(b_ada[1]), "g1pre": vec8(g_mix_pre[1]),
            "w_pool": wpool_c, "pscale": vec8(pool_scale[0]),
            "edge": np.ascontiguousarray(np.broadcast_to(edge.reshape(1, 64), (128, 64))),
            "hmask": np.ascontiguousarray(np.broadcast_to(hmask.reshape(1, 2), (128, 2))),
            "w_qkv": wqkv_c, "gqk": np.ascontiguousarray(np.stack([g_q[0], g_k[0]], axis=1)),
            "cosT": np.ascontiguousarray(np.cos(ang).astype(np.float32)), "sinT": np.ascontiguousarray(np.sin(ang).astype(np.float32)),
            "rot": rot,
        })
        mapsA.append(m)
    if FUSED:
        wo_c = chunk_in(w_o[0], 128)
        gqkrow = np.ascontiguousarray(np.concatenate([g_q[0], g_k[0]]).reshape(1, 256))
        mapsF = []
        for core in range(8):
            m = dict(mapsA[core])
            m.pop("wada1")
            m.update({"wadaB": cmB["wada"], "gainsB": cmB["gains"], "w_inB": cmB["w_in"], "w_outB": cmB["w_out"],
                      "w_o": wo_c, "gqkrow": gqkrow})
            mapsF.append(m)
        resF = run_bass_kernel_spmd(get_nc("F"), mapsF, core_ids=list(range(8))).results
        out = np.empty((B, L, D), np.float32)
        for core in range(8):
            b, j = core // 4, core % 4
            out[b, j * T:(j + 1) * T] = np.asarray(resF[core]["yT"]).T
        return out
    resA = run_bass_kernel_spmd(get_nc("A"), mapsA, core_ids=list(range(8))).results
    _DBG["resA"] = resA
    wo_c = chunk_in(w_o[0], 128)
    mapsB = []
    kall, vall = {}, {}
    for b in range(B):
        ks = [np.asarray(resA[4 * b + j]["kT"]) for j in range(4)]
        vs = [np.asarray(resA[4 * b + j]["v"]) for j in range(4)]
        kall[b] = np.ascontiguousarray(np.concatenate([k[:, :, T:] for k in ks] + [k[:, :, :T] for k in ks], axis=2))
        vcat = np.concatenate([v[T:] for v in vs] + [v[:T] for v in vs], axis=0)
        vall[b] = np.ascontiguousarray(vcat.reshape(8448, 2, 128).transpose(1, 0, 2))
    for core in range(8):
        b = core // 4
        cond = np.stack([vec8(c[b]), vec8(c_ctx)], axis=-1).reshape(128, 16)
        m = dict(cmB)
        m.update({
            "cond": np.ascontiguousarray(cond), "x1T": np.asarray(resA[core]["x1T"]), "qT": np.asarray(resA[core]["qT"]),
            "mods1": np.asarray(resA[core]["mods1"]),
            "kTall": kall[b], "vall": vall[b], "w_o": wo_c,
            "gqkrow": np.ascontiguousarray(np.concatenate([g_q[0], g_k[0]]).reshape(1, 256)),
        })
        mapsB.append(m)
    resB = run_bass_kernel_spmd(get_nc("B"), mapsB, core_ids=list(range(8))).results
    out = np.empty((B, L, D), np.float32)
    for core in range(8):
        b, j = core // 4, core % 4
        out[b, j * T:(j + 1) * T] = np.asarray(resB[core]["yT"]).T
    return out
```

```python
import contextlib
import os
import numpy as np
import ml_dtypes
import concourse.bass as bass
import concourse.mybir as mybir
from concourse.bass_utils import run_bass_kernel_spmd

F32 = mybir.dt.float32
BF16 = mybir.dt.bfloat16
AF = mybir.ActivationFunctionType
ALU = mybir.AluOpType
NPBF = ml_dtypes.bfloat16

ENGS = ("pe", "act", "dve", "pool", "sp")
T = 2048
TC = 64
HO = 8
EPS = 1e-6
NWB = 8


class Op:
    __slots__ = ("eng", "emit", "deps", "dma", "sig", "cnt", "sem", "idx", "nosig")

    def __init__(self, eng, emit, dma):
        self.eng, self.emit, self.dma = eng, emit, dma
        self.nosig = False
        self.deps, self.sig, self.cnt, self.sem = [], False, 0, None


class Sched:
    def __init__(self, n_dma_sems=8):
        self.ops = {e: [] for e in ENGS}
        self.all_ops = []
        self.last_w, self.readers, self.alias_deps = {}, {}, {}
        self.n_dma_sems = n_dma_sems
        self.dma_ring = {e: [] for e in ENGS}

    def alias(self, new_buf, old_bufs):
        s = self.alias_deps.setdefault(new_buf, [])
        olds, seen = set(old_bufs), set(id(o) for o in s)
        for k, w in self.last_w.items():
            if k[0] in olds and id(w) not in seen:
                s.append(w); seen.add(id(w))
        for k, rs in self.readers.items():
            if k[0] in olds:
                for r in rs:
                    if id(r) not in seen:
                        s.append(r); seen.add(id(r))

    def add(self, eng, emit, reads=(), writes=(), dma=False, nosig=False, after=()):
        op = Op(eng, emit, dma)
        op.nosig = nosig
        deps = {id(o): o for o in after}
        for k in reads:
            w = self.last_w.get(k)
            if w is not None:
                deps[id(w)] = w
            self.readers.setdefault(k, []).append(op)
            for o in self.alias_deps.get(k[0], ()):
                deps[id(o)] = o
        for k in writes:
            w = self.last_w.get(k)
            if w is not None:
                deps[id(w)] = w
            for r in self.readers.get(k, ()):
                deps[id(r)] = r
            self.readers[k] = []
            self.last_w[k] = op
            for o in self.alias_deps.get(k[0], ()):
                deps[id(o)] = o
        deps.pop(id(op), None)
        if dma:
            ring = self.dma_ring[eng]
            if len(ring) >= self.n_dma_sems:
                prev = ring[len(ring) - self.n_dma_sems]
                deps[id(prev)] = prev
            ring.append(op)
        best, keep = {}, []
        for d in deps.values():
            if d.dma:
                keep.append(d)
            elif d.eng not in best or d.idx > best[d.eng].idx:
                best[d.eng] = d
        op.deps = keep + list(best.values())
        op.idx = len(self.all_ops)
        self.ops[eng].append(op)
        self.all_ops.append(op)
        return op

    def emit_all(self, nc, st, final_waits=()):
        SAME_OK = ("pe", "act", "dve") if os.environ.get("KSAME", "0") == "1" else ("pe",)

        def skip(d, op):
            return d.eng == op.eng and d.eng in SAME_OK and not d.dma and not op.dma
        for op in self.all_ops:
            for d in op.deps:
                if not skip(d, op):
                    d.sig = True
        for op in final_waits:
            op.sig = True
        for op in self.all_ops:
            assert not (op.sig and op.nosig), "fp32 matmul must not carry a semaphore increment"
        esem = {e: st.enter_context(nc.semaphore(f"s_{e}")) for e in ENGS}
        dsem = {e: [st.enter_context(nc.semaphore(f"d_{e}{i}")) for i in range(self.n_dma_sems)]
                for e in ENGS if self.dma_ring[e]}
        for e in ENGS:
            k, ecnt, dcnt = 0, 0, [0] * self.n_dma_sems
            for op in self.ops[e]:
                if op.dma:
                    i = k % self.n_dma_sems
                    k += 1
                    dcnt[i] += 16
                    op.sem, op.cnt, op.sig = dsem[e][i], dcnt[i], True
                elif op.sig:
                    ecnt += 1
                    op.sem, op.cnt = esem[e], ecnt
        block = st.enter_context(nc.Block())
        engmap = {"pe": block.tensor, "act": block.scalar, "dve": block.vector,
                  "pool": block.gpsimd, "sp": block.sync}

        def make(e):
            def body(eng):
                waited = {}
                for op in self.ops[e]:
                    need = {}
                    for d in op.deps:
                        if skip(d, op):
                            continue
                        key = id(d.sem)
                        if d.cnt > waited.get(key, 0) and d.cnt > need.get(key, (0, None))[0]:
                            need[key] = (d.cnt, d.sem)
                    for key, (cnt, sem) in need.items():
                        eng.wait_ge(sem, cnt)
                        waited[key] = cnt
                    ins = op.emit(eng)
                    if op.sig:
                        ins.then_inc(op.sem, 16 if op.dma else 1)
                if e == "sp":
                    for op in final_waits:
                        eng.wait_ge(op.sem, op.cnt)
            return body

        for e in ENGS:
            if self.ops[e] or e == "sp":
                engmap[e](make(e))


class Prog:
    def __init__(self, mode):
        self.mode = mode
        self.nc = bass.Bass("TRN2", target_bir_lowering=False)
        self.S = Sched(int(os.environ.get("KSEMS", "8")))
        self.st = contextlib.ExitStack()
        self.finals = []
        self.rr = {}

    def din(self, name, shape, dt=F32):
        return self.nc.dram_tensor(name, list(shape), dt, kind="ExternalInput").ap()

    def dout(self, name, shape, dt=F32):
        return self.nc.dram_tensor(name, list(shape), dt, kind="ExternalOutput").ap()

    def sb(self, name, shape, dt=F32):
        return self.st.enter_context(self.nc.sbuf_tensor("sb_" + name, list(shape), dt))

    def ps(self, name):
        return self.st.enter_context(self.nc.psum_tensor(name, [128, 512], F32))

    def nxt(self, name, n):
        i = self.rr.get(name, 0)
        self.rr[name] = i + 1
        return i % n

    def ew(self, name):
        return ("dve", "pool")[self.nxt(name, 2)]

    def load(self, dst, src, wkey, eng="sp", after=()):
        return self.S.add(eng, lambda e: e.dma_start(out=dst, in_=src), writes=[wkey], dma=True, after=after)

    def dint(self, name, shape, dt=F32):
        return self.nc.dram_tensor(name, list(shape), dt, kind="Internal").ap()

    def store(self, dst, src, rkey, eng="sp"):
        op = self.S.add(eng, lambda e: e.dma_start(out=dst, in_=src), reads=[rkey], dma=True)
        self.finals.append(op)
        return op


class _Cut(Exception):
    pass


def _build(mode):
    P = Prog(mode)
    CUT = int(os.environ.get("KCUT", "99"))
    nc, S = P.nc, P.S
    A = mode in ("A", "F")
    Bm = mode in ("B", "F")
    FU = mode == "F"
    DBG = os.environ.get("KDBG") == "1" and not FU
    d_cond = P.din("cond", [128, 16])
    d_wada = P.din("wada", [48, 128, 1024])
    d_bada = P.din("bada", [128, 48])
    d_gains = P.din("gains", [128, 32])
    d_win = P.din("w_in", [32, 128, 1024])
    d_wout = P.din("w_out", [32, 128, 1024])
    if A:
        d_x = P.din("xT", [1024, T + 2 * HO])
        d_ctx = P.din("ctxT", [1024, TC + 2 * HO])
        if not FU:
            d_wada1 = P.din("wada1", [16, 128, 1024])
        d_bada1 = P.din("bada1", [128, 48])
        d_g1pre = P.din("g1pre", [128, 8])
        d_wpool = P.din("w_pool", [2, 128, 1024])
        d_pscale = P.din("pscale", [128, 8])
        d_edge = P.din("edge", [128, 64])
        d_hmask = P.din("hmask", [128, 2])
        d_wqkv = P.din("w_qkv", [12, 128, 1024])
        d_gqk = P.din("gqk", [128, 2])
        d_cos = P.din("cosT", [128, T])
        d_sin = P.din("sinT", [128, T])
        d_rot = P.din("rot", [128, 128])
        if FU:
            o_q = P.dint("q_d", [8, 128, T], BF16)
            k_loc = P.dint("k_loc", [256, T + TC], BF16)
            o_k = k_loc.rearrange("(h d) t -> h d t", h=2)
            o_v = P.dint("v_loc", [T + TC, 256], BF16)
            k_all = P.dint("k_all", [4 * 256, T + TC], BF16)
            v_all = P.dint("v_all", [4 * (T + TC), 256], BF16)
        else:
            o_x1 = P.dout("x1T", [1024, T])
            o_q = P.dout("qT", [8, 128, T], BF16)
            o_k = P.dout("kT", [2, 128, T + TC], BF16)
            o_v = P.dout("v", [T + TC, 256], BF16)
        if DBG:
            g_h2 = P.dout("dbg_h2", [128, 8, 512], BF16)
            g_u = P.dout("dbg_u", [128, 32, 512], BF16)
            g_y = P.dout("dbg_ysb", [128, 8, 512])
            g_w = P.dout("dbg_wb", [128, 1024], BF16)
            g_m = P.dout("dbg_mods", [128, 96])
            g_G = P.dout("dbg_G", [128, 64])
            g_b = P.dout("dbg_bada", [128, 48])
    if Bm:
        if FU:
            d_q = o_q
            d_wadaB = P.din("wadaB", [48, 128, 1024])
            d_gainsB = P.din("gainsB", [128, 32])
            d_winB = P.din("w_inB", [32, 128, 1024])
            d_woutB = P.din("w_outB", [32, 128, 1024])
        else:
            d_x = P.din("x1T", [1024, T])
            d_q = P.din("qT", [8, 128, T], BF16)
            d_k = P.din("kTall", [2, 128, 8448], BF16)
            d_v = P.din("vall", [2, 8448, 128], BF16)
            d_wadaB, d_winB, d_woutB = d_wada, d_win, d_wout
        d_wo = P.din("w_o", [8, 128, 1024])
        d_gqkrow = P.din("gqkrow", [1, 256])
        o_y = P.dout("yT", [1024, T])
        if DBG:
            g_ot = P.dout("dbg_ot", [128, 8, 512], BF16)
            g_yw = P.dout("dbg_yw", [128, 8, 512])
            g_xm = P.dout("dbg_xm", [128, 8, 512])
            g_m = P.dout("dbg_mods", [128, 96])
            g_nc = P.dout("dbg_negc", [128, 1])

    XW = T + 2 * HO
    XOFF = HO if A else 0
    X = P.sb("X", [128, 8, XW if A else T])
    BIG = P.sb("BIG", [128, 17408])
    MID = P.sb("MID", [128, 8704])
    STG = [P.sb(f"stg{i}", [128, 1024]) for i in range(2)]
    WB = [P.sb(f"wb{i}", [128, 1024], BF16) for i in range(NWB)]
    rstd = [P.sb(f"rstd{i}", [128, 512]) for i in range(2)]
    tmp = [P.sb(f"tmp{i}", [128, 512]) for i in range(2)]
    sq = [P.sb(f"sq{i}", [128, 512], BF16) for i in range(2)]
    ones = P.sb("ones", [128, 128], BF16)
    cond = P.sb("cond", [128, 8, 2])
    condb = P.sb("condb", [128, 8, 2], BF16)
    mods = P.sb("mods", [128, 48, 2])
    bada = P.sb("bada", [128, 48])
    gains = P.sb("gains", [128, 4, 8])
    G1 = P.sb("G1", [128, 8, 2]); Gp1 = P.sb("Gp1", [128, 8, 2])
    G2 = P.sb("G2", [128, 8, 2]); Gp2 = P.sb("Gp2", [128, 8, 2])
    PSALL = P.st.enter_context(nc.psum_tensor("psall", [128, 4096], F32))
    psb = [PSALL[:, i * 512:(i + 1) * 512] for i in range(8)]
    u = BIG[:].bitcast(BF16).rearrange("p (c t) -> p c t", c=32)
    h2 = MID[:].bitcast(BF16)[:, 0:8 * 1088].rearrange("p (c t) -> p c t", c=8)
    ysb = MID[:, 0:8 * 1088].rearrange("p (c t) -> p c t", c=8)
    if A:
        XC = P.sb("XC", [128, 8, TC + 2 * HO])
        pscale = P.sb("pscale", [128, 8])
        edge = P.sb("edge", [128, 4, 16])
        hmask = P.sb("hmask", [128, 2])
        gqk = P.sb("gqk", [128, 2])
        mods1 = P.sb("mods1", [128, 16, 2])
        bada1 = P.sb("bada1", [128, 48])
        g1pre = P.sb("g1pre", [128, 8])
        G1b = P.sb("G1b", [128, 8, 2])
        hbuf = BIG[:, 0:8 * XW].rearrange("p (c t) -> p c t", c=8)
        diff = MID[:].bitcast(BF16)[:, 0:2048].rearrange("p (c t) -> p c t", c=8)
        ym = MID[:, 1024:1024 + 2048].rearrange("p (c t) -> p c t", c=8)
        pscr = [MID[:, 3072 + i * 272: 3072 + (i + 1) * 272] for i in range(4)]
        wpool = MID[:, 4160:4160 + 1024].bitcast(BF16)
        HC = BIG[:, 8 * XW: 8 * XW + 8 * (TC + 2 * HO)].rearrange("p (c t) -> p c t", c=8)
        ostg = [BIG[:, 2 * T + 1024 + i * 256: 2 * T + 1024 + (i + 1) * 256].bitcast(BF16) for i in range(2)]
        rot = BIG[:, 2 * T + 1536: 2 * T + 1536 + 128]
    if Bm:
        gainsB = P.sb("gainsB", [128, 4, 8]) if FU else gains
        gqkrow = P.sb("gqkrow", [1, 256])
        negc = P.sb("negc", [128, 1])
        c1 = P.sb("c1", [1, 4])
        ones32 = P.sb("ones32", [1, 128])
        BB = BIG[:].bitcast(BF16)
        KT = BB[:, 0:8448]
        V = BB[:, 8448:2 * 8448].rearrange("p (k d) -> p k d", k=66)
        OTall = BB[:, 2 * 8448:2 * 8448 + 8 * T].rearrange("p (h t) -> p h t", h=8)
        MB = MID[:].bitcast(BF16)
        QTg = [MB[:, i * 2048:(i + 1) * 2048].rearrange("p (h t) -> p h t", h=4) for i in range(2)]
        yw = MID[:, 0:4096].rearrange("p (c t) -> p c t", c=8)
        Pb = [MB[:, 4096 + i * 1024: 4096 + (i + 1) * 1024] for i in range(3)]
        Pz = [MB[:, 7168 + i * 512: 7168 + (i + 1) * 512] for i in range(3)]

    xv = d_x.rearrange("(c p) t -> p c t", p=128)
    for kc in range(8):
        P.load(X[:, kc, 0:(XW if A else T)], xv[:, kc, :], ("X", kc))
    if A:
        cv = d_ctx.rearrange("(c p) t -> p c t", p=128)
        P.load(XC[:], cv, ("XC",))
    S.add("pool", lambda e: e.memset(ones[:], 1.0), writes=[("ones",)])
    P.load(cond[:].rearrange("p c v -> p (c v)"), d_cond, ("cond",))
    P.load(bada[:], d_bada, ("bada",))
    P.load(gains[:].rearrange("p a c -> p (a c)"), d_gains, ("gains",))
    S.add("act", lambda e: e.activation(out=condb[:], in_=cond[:], func=AF.Silu),
          reads=[("cond",)], writes=[("condb",)])

    def stage(src):
        i = P.nxt("stg", 2)
        P.load(STG[i][:], src, ("stg", i))
        return i

    def cast(si):
        i = P.nxt("wb", NWB)
        eng = ("act", "dve")[P.nxt("casteng", 2)]
        if eng == "act":
            S.add("act", lambda e: e.copy(out=WB[i][:], in_=STG[si][:]), reads=[("stg", si)], writes=[("wb", i)])
        else:
            S.add("dve", lambda e: e.tensor_copy(out=WB[i][:], in_=STG[si][:]), reads=[("stg", si)], writes=[("wb", i)])
        return i

    def wstream(chunks, depth):
        st_ = {"next": 0, "wi": {}}

        def get(i):
            while st_["next"] <= min(i + depth, len(chunks) - 1):
                k = st_["next"]
                st_["wi"][k] = cast(stage(chunks[k]))
                st_["next"] += 1
            return st_["wi"][i]
        return get

    PS_MODS = 7

    def pe_marker(reads, writes):
        S.add("pe", lambda e: e.matmul(psb[PS_MODS][:, 510:512], lhsT=ones[:], rhs=ones[:, 0:2], start=True, stop=True),
              reads=list(reads) + [("ones",)], writes=list(writes) + [("ps_mark",)])

    def ada_mm(dw, chunks, oc0):
        pm = psb[PS_MODS]
        wget = wstream([dw[ci] for ci in chunks], 3)
        for idx, ci in enumerate(chunks):
            wi = wget(idx)
            oc = ci - oc0
            for kc in range(8):
                S.add("pe", lambda e, wi=wi, kc=kc, oc=oc: e.matmul(
                    pm[:, 2 * oc:2 * oc + 2], lhsT=WB[wi][:, kc * 128:(kc + 1) * 128],
                    rhs=condb[:, kc, :], start=(kc == 0), stop=(kc == 7)),
                    reads=[("wb", wi), ("condb",)], writes=[("ps", PS_MODS)])

    def ada_evac(lo, hi, mods_t, mkey, bada_t, bkey, oc0):
        pm = psb[PS_MODS]
        a, b_ = lo - oc0, hi - oc0
        for v in range(2):
            S.add("dve", lambda e, v=v: e.tensor_tensor(
                out=mods_t[:, a:b_, v], in0=pm[:, 2 * a:2 * b_].rearrange("p (o v) -> p o v", v=2)[:, :, v],
                in1=bada_t[:, lo:hi], op=ALU.add),
                reads=[("ps", PS_MODS), bkey], writes=[mkey])

    def ada(dw, chunks, mods_t, mkey, bada_t, bkey, oc0):
        ada_mm(dw, chunks, oc0)
        ada_evac(chunks[0], chunks[-1] + 1, mods_t, mkey, bada_t, bkey, oc0)

    def combine(out_t, okey, gain_ap, gkey, mods_t, mkey, mi, plus_one):
        for v in range(2):
            if plus_one:
                S.add("dve", lambda e, v=v: e.scalar_tensor_tensor(
                    out=out_t[:, :, v], in0=mods_t[:, mi * 8:(mi + 1) * 8, v], scalar=1.0, in1=gain_ap,
                    op0=ALU.add, op1=ALU.mult), reads=[mkey, gkey], writes=[okey])
            else:
                S.add("dve", lambda e, v=v: e.tensor_tensor(
                    out=out_t[:, :, v], in0=mods_t[:, mi * 8:(mi + 1) * 8, v], in1=gain_ap, op=ALU.mult),
                    reads=[mkey, gkey], writes=[okey])

    PS_STAT = 6

    def rms_rstd(src_fn, src_keys, n, scale_ap=None):
        pst = psb[PS_STAT]
        for kc in range(8):
            qi = P.nxt("sq", 2)
            S.add("act", lambda e, kc=kc, qi=qi: e.activation(out=sq[qi][:, :n], in_=src_fn(kc), func=AF.Square),
                  reads=[src_keys(kc)], writes=[("sq", qi)])
            S.add("pe", lambda e, kc=kc, qi=qi: e.matmul(pst[:, :n], lhsT=ones[:], rhs=sq[qi][:, :n],
                                                         start=(kc == 0), stop=(kc == 7)),
                  reads=[("sq", qi), ("ones",)], writes=[("ps", PS_STAT)])
        r = P.nxt("rstd", 2)
        S.add("act", lambda e: e.activation(out=rstd[r][:, :n], in_=pst[:, :n], func=AF.Ln, bias=EPS, scale=1.0 / 1024),
              reads=[("ps", PS_STAT)], writes=[("rstd", r)])
        S.add("act", lambda e: e.activation(out=rstd[r][:, :n], in_=rstd[r][:, :n], func=AF.Exp, scale=-0.5),
              reads=[("rstd", r)], writes=[("rstd", r)])
        return r

    def modulate(src_fn, src_keys, n, r, Gt, gk, sht, shk, shi, v, dst_fn, dst_keys):
        for kc in range(8):
            ti = P.nxt("tmp", 2)
            S.add("dve", lambda e, kc=kc, ti=ti: e.tensor_tensor(out=tmp[ti][:, :n], in0=src_fn(kc), in1=rstd[r][:, :n], op=ALU.mult),
                  reads=[src_keys(kc), ("rstd", r)], writes=[("tmp", ti)])
            S.add("act", lambda e, kc=kc, ti=ti: e.activation(out=dst_fn(kc), in_=tmp[ti][:, :n], func=AF.Identity,
                                                              bias=sht[:, shi * 8 + kc, v:v + 1], scale=Gt[:, kc, v:v + 1]),
                  reads=[("tmp", ti), gk, shk], writes=[dst_keys(kc)])

    def resid_add(xfn, xkeys, yfn, ykeys, n, r, Gpt, gpk, v):
        for kc in range(8):
            ti = P.nxt("tmp", 2)
            S.add("dve", lambda e, kc=kc, ti=ti: e.scalar_tensor_tensor(
                out=tmp[ti][:, :n], in0=yfn(kc), scalar=Gpt[:, kc, v:v + 1], in1=rstd[r][:, :n], op0=ALU.mult, op1=ALU.mult),
                reads=[ykeys(kc), gpk, ("rstd", r)], writes=[("tmp", ti)])
            S.add("pool", lambda e, kc=kc, ti=ti: e.tensor_tensor(out=xfn(kc), in0=xfn(kc), in1=tmp[ti][:, :n], op=ALU.add),
                  reads=[("tmp", ti), xkeys(kc)], writes=[xkeys(kc)])

    dbg_state = {"n": 0}

    def mlp(groups, d_win=d_win, d_wout=d_wout):
        dbg = A and DBG and dbg_state["n"] == 0
        dbg_state["n"] += 1
        S.alias("h2", ["ysb", "diff", "ym", "pscr", "QT", "Pb", "Pz", "wpool", "yw"])
        S.alias("u", ["hbuf", "HC", "KV", "OTall", "rope"])
        for (xfn, xkey, n, v, uc) in groups:
            r = rms_rstd(xfn, xkey, n)
            modulate(xfn, xkey, n, r, G2, ("G2",), mods, ("mods",), 3, v,
                     lambda kc, uc=uc, n=n: h2[:, kc, uc:uc + n], lambda kc, uc=uc: ("h2", kc, uc))
        if dbg:
            P.store(g_m, mods[:].rearrange("p o v -> p (o v)"), ("mods",))
            P.store(g_b, bada[:], ("bada",))
            for i_, (Gt_, gk_) in enumerate(((G1, ("G1",)), (Gp1, ("Gp1",)), (G2, ("G2",)), (Gp2, ("Gp2",)))):
                P.store(g_G[:, i_ * 16:(i_ + 1) * 16], Gt_[:].rearrange("p c v -> p (c v)"), gk_)
            for kc in range(8):
                P.store(g_h2[:, kc, :], h2[:, kc, 0:512], ("h2", kc, 0))
        wget = wstream([d_win[i] for i in range(32)] + [d_wout[i] for i in range(32)], 4)
        for oc in range(32):
            wi = wget(oc)
            if dbg and oc == 5:
                P.store(g_w, WB[wi][:], ("wb", wi))
            for (xfn, xkey, n, v, uc) in groups:
                b = P.nxt("psmain", 4)
                for kc in range(8):
                    S.add("pe", lambda e, b=b, kc=kc, wi=wi, uc=uc, n=n: e.matmul(
                        psb[b][:, :n], lhsT=WB[wi][:, kc * 128:(kc + 1) * 128],
                        rhs=h2[:, kc, uc:uc + n], start=(kc == 0), stop=(kc == 7)),
                        reads=[("wb", wi), ("h2", kc, uc)], writes=[("ps", b)])
                ti = P.nxt("tmp", 2)
                S.add("act", lambda e, b=b, ti=ti, n=n: e.activation(out=tmp[ti][:, :n], in_=psb[b][:, :n], func=AF.Relu),
                      reads=[("ps", b)], writes=[("tmp", ti)])
                S.add("dve", lambda e, ti=ti, oc=oc, uc=uc, n=n: e.tensor_tensor(
                    out=u[:, oc, uc:uc + n], in0=tmp[ti][:, :n], in1=tmp[ti][:, :n], op=ALU.mult),
                    reads=[("tmp", ti)], writes=[("u", oc, uc)])
        if dbg:
            for kc in range(32):
                P.store(g_u[:, kc, :], u[:, kc, 0:512], ("u", kc, 0))
        S.alias("ysb", ["h2"])
        for oc in range(8):
            wis = [wget(32 + 4 * oc + q) for q in range(4)]
            for (xfn, xkey, n, v, uc) in groups:
                b = P.nxt("psmain", 4)
                for kc in range(32):
                    wi = wis[kc // 8]
                    S.add("pe", lambda e, b=b, kc=kc, wi=wi, uc=uc, n=n: e.matmul(
                        psb[b][:, :n], lhsT=WB[wi][:, (kc % 8) * 128:(kc % 8 + 1) * 128],
                        rhs=u[:, kc, uc:uc + n], start=(kc == 0), stop=(kc == 31)),
                        reads=[("wb", wi), ("u", kc, uc)], writes=[("ps", b)])
                S.add("dve", lambda e, b=b, oc=oc, uc=uc, n=n: e.tensor_copy(out=ysb[:, oc, uc:uc + n], in_=psb[b][:, :n]),
                      reads=[("ps", b)], writes=[("ysb", oc, uc)])
        if dbg:
            for kc in range(8):
                P.store(g_y[:, kc, :], ysb[:, kc, 0:512], ("ysb", kc, 0))
        for (xfn, xkey, n, v, uc) in groups:
            yfn = lambda kc, uc=uc, n=n: ysb[:, kc, uc:uc + n]
            ykey = lambda kc, uc=uc: ("ysb", kc, uc)
            r = rms_rstd(yfn, ykey, n)
            resid_add(xfn, xkey, yfn, ykey, n, r, Gp2, ("Gp2",), v)

    def xmain(g):
        off = HO if A else 0
        return (lambda kc: X[:, kc, off + g * 512: off + (g + 1) * 512]), (lambda kc: ("X", kc, off + g * 512))

    S.alias_deps["X"] = [S.last_w[("X", kc)] for kc in range(8)]
    if A:
        S.alias_deps["XC"] = []

    if A:
        P.load(pscale[:], d_pscale, ("pscale",))
        P.load(edge[:].rearrange("p a c -> p (a c)"), d_edge, ("edge",))
        P.load(hmask[:], d_hmask, ("hmask",))
        P.load(gqk[:], d_gqk, ("gqk",))
        P.load(bada1[:], d_bada1, ("bada1",))
        P.load(g1pre[:], d_g1pre, ("g1pre",))
        for i in range(2):
            wpi = stage(d_wpool[i])
            S.add("dve", lambda e, i=i, wpi=wpi: e.tensor_copy(out=wpool[:, i * 1024:(i + 1) * 1024], in_=STG[wpi][:]),
                  reads=[("stg", wpi)], writes=[("wpool", i)])
        ada(d_wada, list(range(16)), mods, ("mods", 0), bada, ("bada",), 0)
        combine(G1, ("G1",), gains[:, 0, :], ("gains",), mods, ("mods", 0), 1, True)

        def cut(k):
            if CUT <= k and not FU:
                x1v_ = o_x1.rearrange("(c p) t -> p c t", p=128)
                for kc_ in range(8):
                    S.add("sp", lambda e, kc_=kc_: e.dma_start(out=x1v_[:, kc_, :], in_=X[:, kc_, HO:HO + T]),
                          reads=[("X", kc_, HO + g_ * 512) for g_ in range(4)], dma=True)
                    P.finals.append(S.all_ops[-1])
                if DBG:
                    P.store(g_m, mods[:].rearrange("p o v -> p (o v)"), ("mods",))
                    S.add("dve", lambda e: e.tensor_copy(out=tmp[0][:, 0:96], in_=psb[PS_MODS][:, 0:96]),
                          reads=[("ps", PS_MODS)], writes=[("tmp", 0)])
                    P.store(g_y[:, 0, 0:96], tmp[0][:, 0:96], ("tmp", 0))
                raise _Cut(P)
        def hgroups(src, skey, dst, dkey, width, v):
            segs = [(0, HO, 0), (width - HO, HO, 1)]
            c0 = HO
            while c0 < width - HO:
                n = min(512, width - HO - c0)
                segs.append((c0, n, None))
                c0 += n
            for (c0, n, mk) in segs:
                sfn = lambda kc, c0=c0, n=n: src[:, kc, c0:c0 + n]
                sk = lambda kc, c0=c0: (skey, kc, c0)
                dfn = lambda kc, c0=c0, n=n: dst[:, kc, c0:c0 + n]
                dk = lambda kc, c0=c0: (dkey, kc, c0)
                r = rms_rstd(sfn, sk, n)
                modulate(sfn, sk, n, r, G1, ("G1",), mods, ("mods", 0), 0, v, dfn, dk)
                if mk is not None:
                    for kc in range(8):
                        S.add("dve", lambda e, kc=kc, c0=c0, n=n, mk=mk: e.tensor_scalar(
                            out=dst[:, kc, c0:c0 + n], in0=dst[:, kc, c0:c0 + n], scalar1=hmask[:, mk:mk + 1], scalar2=None, op0=ALU.mult),
                            reads=[("hmask",), dk(kc)], writes=[dk(kc)])
        cut(1)
        S.alias_deps["XCs"] = [S.last_w[("XC",)]]
        hgroups(X, "X", hbuf, "hbuf", XW, 0)
        hgroups(XC, "XCs", HC, "HC", TC + 2 * HO, 1)

        def hkeys(dkey, kc, lo, hi, width):
            ks = []
            segs = [(0, HO), (width - HO, HO)]
            c0 = HO
            while c0 < width - HO:
                n = min(512, width - HO - c0)
                segs.append((c0, n)); c0 += n
            for (s0, n) in segs:
                if s0 < hi and s0 + n > lo:
                    ks.append((dkey, kc, s0))
            return ks

        def pool_group(hb, hkey, width, c0, n, v, xsrc, xkey, left_edge, right_edge):
            S.alias("diff", ["h2", "ysb"]); S.alias("ym", ["h2", "ysb"]); S.alias("pscr", ["h2", "ysb"])
            base = c0 - HO
            for g, w in enumerate((2, 4, 8, 16)):
                for kc in (2 * g, 2 * g + 1):
                    hh = lambda a, b, kc=kc: hb[:, kc, base + a: base + b]
                    hk = hkeys(hkey, kc, base, base + n + 16, width)
                    pa, pb_, pc, pS = pscr
                    eng = P.ew("pool_eng")

                    def tt(out, i0, i1, reads, writes, eng=eng):
                        S.add(eng, lambda e: e.tensor_tensor(out=out, in0=i0, in1=i1, op=ALU.add), reads=reads, writes=writes)
                    kA, kB, kC, kS = ("pscr", 0), ("pscr", 1), ("pscr", 2), ("pscr", 3)
                    if w == 2:
                        tt(pS[:, :n], hh(7, 7 + n), hh(8, 8 + n), hk, [kS])
                    else:
                        tt(pa[:, :n + 15], hh(0, n + 15), hh(1, n + 16), hk, [kA])
                        if w == 4:
                            tt(pS[:, :n], pa[:, 6:6 + n], pa[:, 8:8 + n], [kA], [kS])
                        else:
                            tt(pb_[:, :n + 13], pa[:, 0:n + 13], pa[:, 2:n + 15], [kA], [kB])
                            if w == 8:
                                tt(pS[:, :n], pb_[:, 4:4 + n], pb_[:, 8:8 + n], [kB], [kS])
                            else:
                                tt(pc[:, :n + 9], pb_[:, 0:n + 9], pb_[:, 4:n + 13], [kB], [kC])
                                tt(pS[:, :n], pc[:, 0:n], pc[:, 8:8 + n], [kC], [kS])
                    S.add("dve", lambda e, kc=kc, w=w: e.scalar_tensor_tensor(
                        out=diff[:, kc, :n], in0=pS[:, :n], scalar=1.0 / w, in1=hb[:, kc, c0:c0 + n], op0=ALU.mult, op1=ALU.subtract),
                        reads=[kS] + hk, writes=[("diff", kc)])
                    for (flag, cs, ei) in ((left_edge, 0, 0), (right_edge, n - 8, 8)):
                        if flag:
                            S.add("dve", lambda e, cs=cs, ei=ei, g=g: e.tensor_tensor(
                                out=pS[:, cs:cs + 8], in0=pS[:, cs:cs + 8], in1=edge[:, g, ei:ei + 8], op=ALU.mult),
                                reads=[kS, ("edge",)], writes=[kS])
                            S.add("dve", lambda e, cs=cs, kc=kc: e.tensor_tensor(
                                out=diff[:, kc, cs:cs + 8], in0=pS[:, cs:cs + 8], in1=hb[:, kc, c0 + cs:c0 + cs + 8], op=ALU.subtract),
                                reads=[kS] + hk, writes=[("diff", kc)])
                for ol in range(2):
                    oc = 2 * g + ol
                    b = P.nxt("psmain", 4)
                    for kl in range(2):
                        S.add("pe", lambda e, b=b, g=g, kl=kl, ol=ol: e.matmul(
                            psb[b][:, :n], lhsT=wpool[:, g * 512 + kl * 256 + ol * 128: g * 512 + kl * 256 + (ol + 1) * 128],
                            rhs=diff[:, 2 * g + kl, :n], start=(kl == 0), stop=(kl == 1)),
                            reads=[("wpool", g // 2), ("diff", 2 * g + kl)], writes=[("ps", b)])
                    S.add("dve", lambda e, b=b, oc=oc: e.tensor_scalar(
                        out=ym[:, oc, :n], in0=psb[b][:, :n], scalar1=pscale[:, oc:oc + 1], scalar2=None, op0=ALU.mult),
                        reads=[("ps", b), ("pscale",)], writes=[("ym", oc)])
            yfn = lambda kc: ym[:, kc, :n]
            ykey = lambda kc: ("ym", kc)
            r = rms_rstd(yfn, ykey, n)
            resid_add(xsrc, xkey, yfn, ykey, n, r, Gp1, ("Gp1",), v)

        cut(2)
        S.alias("hbuf", ["u"]); S.alias("HC", ["u"])
        ada(d_wada, list(range(16, 24)), mods, ("mods", 1), bada, ("bada",), 0)
        combine(Gp1, ("Gp1",), gains[:, 1, :], ("gains",), mods, ("mods", 1), 2, False)
        for g in range(8):
            c0 = HO + g * 256
            pool_group(hbuf, "hbuf", XW, c0, 256, 0,
                       lambda kc, c0=c0: X[:, kc, c0:c0 + 256], lambda kc, c0=c0: ("X", kc, HO + ((c0 - HO) // 512) * 512), g == 0, g == 7)
            ada_mm(d_wada, list(range(24 + 3 * g, 27 + 3 * g)), 0)
        ada_evac(24, 48, mods, ("mods",), bada, ("bada",), 0)
        combine(G2, ("G2",), gains[:, 2, :], ("gains",), mods, ("mods",), 4, True)
        combine(Gp2, ("Gp2",), gains[:, 3, :], ("gains",), mods, ("mods",), 5, False)
        pool_group(HC, "HC", TC + 2 * HO, HO, TC, 1,
                   lambda kc: XC[:, kc, HO:HO + TC], lambda kc: ("XCs", kc, HO), True, True)

        cut(3)
        def lat_group(g, uc):
            c0 = HO + g * 512
            return (lambda kc, c0=c0: X[:, kc, c0:c0 + 512], lambda kc, c0=c0: ("X", kc, c0), 512, 0, uc)
        ctx_group = (lambda kc: XC[:, kc, HO:HO + TC], lambda kc: ("XCs", kc, HO), TC, 1, 1024)
        mlp([lat_group(0, 0), lat_group(1, 512), ctx_group])
        mlp([lat_group(2, 0), lat_group(3, 512)])

        cut(4)
        if not FU:
            x1v = o_x1.rearrange("(c p) t -> p c t", p=128)
            for kc in range(8):
                P.S.add("sp", lambda e, kc=kc: e.dma_start(out=x1v[:, kc, :], in_=X[:, kc, HO:HO + T]),
                        reads=[("X", kc, HO + g * 512) for g in range(4)], dma=True)
                P.finals.append(S.all_ops[-1])
        qstores, kstores, vstores = [], [], []

        ada(d_wadaB if FU else d_wada1, list(range(16)), mods1, ("mods1",), bada1, ("bada1",), 0)
        combine(G1b, ("G1b",), g1pre[:], ("g1pre",), mods1, ("mods1",), 1, True)
        h1 = MID[:].bitcast(BF16)[:, 0:8 * (T + TC)].rearrange("p (c t) -> p c t", c=8)
        S.alias("h1", ["h2", "ysb", "diff", "ym", "pscr", "wpool"])
        S.alias("rope", ["u", "hbuf"])
        cosT = BIG[:, 0:T]; sinT = BIG[:, T:2 * T]
        qn = [BIG[:, 2 * T + i * 512: 2 * T + (i + 1) * 512] for i in range(2)]
        P.load(rot, d_rot, ("rope", "rot"))
        P.load(cosT, d_cos, ("rope", "cos"))
        P.load(sinT, d_sin, ("rope", "sin"))
        grp1 = [(lambda kc, g=g: X[:, kc, HO + g * 512:HO + (g + 1) * 512], lambda kc, g=g: ("X", kc, HO + g * 512), 512, 0, g * 512) for g in range(4)]
        grp1.append((lambda kc: XC[:, kc, HO:HO + TC], lambda kc: ("XCs", kc, HO), TC, 1, T))
        for (xfn, xkey, n, v, uc) in grp1:
            r = rms_rstd(xfn, xkey, n)
            modulate(xfn, xkey, n, r, G1b, ("G1b",), mods1, ("mods1",), 0, v,
                     lambda kc, uc=uc, n=n: h1[:, kc, uc:uc + n], lambda kc, uc=uc: ("h1", kc, uc))
        cut(5)
        items = []
        for hd in range(10):
            for grp in grp1:
                if hd < 8 and grp[3] == 1:
                    continue
                items.append((hd, grp))
        qkvget = wstream([d_wqkv[i] for i in range(12)], 2)

        def make_item(hd, grp):
            (xfn, xkey, n, v, uc) = grp
            isq = hd < 8
            gcol = 0 if isq else 1
            st_ = {}

            def s0():
                wi = qkvget(hd)
                b = st_["b"] = P.nxt("psmain", 4)
                for kc in range(8):
                    S.add("pe", lambda e, b=b, kc=kc, wi=wi: e.matmul(
                        psb[b][:, :n], lhsT=WB[wi][:, kc * 128:(kc + 1) * 128],
                        rhs=h1[:, kc, uc:uc + n], start=(kc == 0), stop=(kc == 7)),
                        reads=[("wb", wi), ("h1", kc, uc)], writes=[("ps", b)])

            def s1():
                b = st_["b"]
                qi = st_["qi"] = P.nxt("sq", 2)
                S.add("act", lambda e: e.activation(out=sq[qi][:, :n], in_=psb[b][:, :n], func=AF.Square),
                      reads=[("ps", b)], writes=[("sq", qi)])

            def s2():
                qi = st_["qi"]
                pb = st_["pb"] = (4, 5)[P.nxt("qkstat", 2)]
                S.add("pe", lambda e: e.matmul(psb[pb][:, :n], lhsT=ones[:], rhs=sq[qi][:, :n], start=True, stop=True),
                      reads=[("sq", qi), ("ones",)], writes=[("ps", pb)])

            def s3():
                pb = st_["pb"]
                r = st_["r"] = P.nxt("rstd", 2)
                S.add("act", lambda e: e.activation(out=rstd[r][:, :n], in_=psb[pb][:, :n], func=AF.Ln, bias=EPS, scale=1.0 / 128),
                      reads=[("ps", pb)], writes=[("rstd", r)])
                S.add("act", lambda e: e.activation(out=rstd[r][:, :n], in_=rstd[r][:, :n], func=AF.Exp, scale=-0.5),
                      reads=[("rstd", r)], writes=[("rstd", r)])

            def s4():
                b, r = st_["b"], st_["r"]
                qj = st_["qj"] = P.nxt("qn", 2)
                S.add("dve", lambda e: e.scalar_tensor_tensor(
                    out=qn[qj][:, :n], in0=psb[b][:, :n], scalar=gqk[:, gcol:gcol + 1], in1=rstd[r][:, :n], op0=ALU.mult, op1=ALU.mult),
                    reads=[("ps", b), ("gqk",), ("rstd", r)], writes=[("rope", "qn", qj)])

            def s5():
                if v != 0:
                    return
                qj = st_["qj"]
                b2 = st_["b2"] = P.nxt("psmain", 4)
                S.add("pe", lambda e: e.matmul(psb[b2][:, :n], lhsT=rot, rhs=qn[qj][:, :n], start=True, stop=True),
                      reads=[("rope", "rot"), ("rope", "qn", qj)], writes=[("ps", b2)], nosig=True)
                pe_marker([("rope", "rot"), ("rope", "qn", qj)], [("ps", b2)])

            def s6():
                if v != 0:
                    return
                qj, b2 = st_["qj"], st_["b2"]
                ti = st_["ti"] = P.nxt("tmp", 2)
                S.add("dve", lambda e: e.tensor_tensor(out=tmp[ti][:, :n], in0=psb[b2][:, :n], in1=sinT[:, uc:uc + n], op=ALU.mult),
                      reads=[("ps", b2), ("rope", "sin")], writes=[("tmp", ti)])
                S.add("pool", lambda e: e.tensor_tensor(out=qn[qj][:, :n], in0=qn[qj][:, :n], in1=cosT[:, uc:uc + n], op=ALU.mult),
                      reads=[("rope", "qn", qj), ("rope", "cos")], writes=[("rope", "qn", qj)])

            def s7():
                qj = st_["qj"]
                oi = P.nxt("ostg", 2)
                if v == 0:
                    ti = st_["ti"]
                    S.add("dve", lambda e: e.tensor_tensor(out=ostg[oi][:, :n], in0=qn[qj][:, :n], in1=tmp[ti][:, :n], op=ALU.add),
                          reads=[("rope", "qn", qj), ("tmp", ti)], writes=[("rope", "ostg", oi)])
                else:
                    S.add("dve", lambda e: e.tensor_copy(out=ostg[oi][:, :n], in_=qn[qj][:, :n]),
                          reads=[("rope", "qn", qj)], writes=[("rope", "ostg", oi)])
                dst = o_q[hd, :, uc:uc + n] if isq else o_k[hd - 8, :, uc:uc + n]
                (qstores if isq else kstores).append(P.store(dst, ostg[oi][:, :n], ("rope", "ostg", oi)))
            return [s0, s1, s2, s3, s4, s5, s6, s7]

        WAVE = 2
        for i0 in range(0, len(items), WAVE):
            batch = [make_item(*it) for it in items[i0:i0 + WAVE]]
            for si in range(8):
                for stg_ in batch:
                    stg_[si]()
        wv = [qkvget(10 + i) for i in range(2)]
        tiles = [(t0, 128) for t0 in range(0, T, 128)] + [(T, TC)]
        for (t0, m) in tiles:
            b = P.nxt("psmain", 4)
            for i in range(2):
                for kc in range(8):
                    S.add("pe", lambda e, b=b, kc=kc, t0=t0, m=m, i=i: e.matmul(
                        psb[b][:m, i * 128:(i + 1) * 128], lhsT=h1[:, kc, t0:t0 + m], rhs=WB[wv[i]][:, kc * 128:(kc + 1) * 128],
                        start=(kc == 0), stop=(kc == 7)),
                        reads=[("wb", wv[i])] + [("h1", kc, (t0 // 512) * 512)], writes=[("ps", b)])
            oi = P.nxt("ostg", 2)
            S.add("dve", lambda e, b=b, oi=oi, m=m: e.tensor_copy(out=ostg[oi][:m, :256], in_=psb[b][:m, :256]),
                  reads=[("ps", b)], writes=[("rope", "ostg", oi)])
            vstores.append(P.store(o_v[t0:t0 + m, :], ostg[oi][:m, :256], ("rope", "ostg", oi)))
        if FU:
            RG = [[0, 1, 2, 3], [4, 5, 6, 7]]
            cck = S.add("pool", lambda e: e.collective_compute("AllGather", ALU.bypass, replica_groups=RG, ins=[k_loc], outs=[k_all]),
                        writes=[("kall",)], dma=True, after=kstores)
            ccv = S.add("pool", lambda e: e.collective_compute("AllGather", ALU.bypass, replica_groups=RG, ins=[o_v], outs=[v_all]),
                        writes=[("vall",)], dma=True, after=vstores)

    if Bm:
        if FU:
            oldB = ["u", "hbuf", "HC", "rope"]
            S.alias("KV", oldB); S.alias("OTall", oldB)
            S.alias("QT", ["h1", "h2", "ysb", "diff", "ym", "pscr", "wpool"])
            P.load(gainsB[:].rearrange("p a c -> p (a c)"), d_gainsB, ("gainsB",))
        P.load(gqkrow[:], d_gqkrow, ("gqkrow",))
        S.add("dve", lambda e: e.tensor_reduce(out=c1[:, 0:2], in_=gqkrow[:].rearrange("p (a d) -> p a d", a=2),
                                                axis=mybir.AxisListType.X, op=ALU.max, apply_absolute_value=True),
              reads=[("gqkrow",)], writes=[("c1",)])
        S.add("dve", lambda e: e.scalar_tensor_tensor(out=c1[:, 2:3], in0=c1[:, 0:1], scalar=-(128.0 ** 0.5), in1=c1[:, 1:2], op0=ALU.mult, op1=ALU.mult),
              reads=[("c1",)], writes=[("c1",)])
        S.add("pool", lambda e: e.memset(ones32[:], 1.0), writes=[("ones32",)])
        S.add("pe", lambda e: e.matmul(psb[PS_MODS][:, 100:101], lhsT=ones32[:], rhs=c1[:, 2:3], start=True, stop=True),
              reads=[("ones32",), ("c1",)], writes=[("ps", PS_MODS)], nosig=True)
        pe_marker([("ones32",), ("c1",)], [("ps", PS_MODS)])
        S.add("dve", lambda e: e.tensor_copy(out=negc[:], in_=psb[PS_MODS][:, 100:101]), reads=[("ps", PS_MODS)], writes=[("negc",)])
        gkB = ("gainsB",) if FU else ("gains",)

        def mods_layer1():
            ada(d_wadaB, list(range(16, 48)), mods[:, 16:48, :], ("mods",), bada1 if FU else bada, ("bada1",) if FU else ("bada",), 16)
            combine(Gp1, ("Gp1",), gainsB[:, 1, :], gkB, mods, ("mods",), 2, False)
            combine(G2, ("G2",), gainsB[:, 2, :], gkB, mods, ("mods",), 4, True)
            combine(Gp2, ("Gp2",), gainsB[:, 3, :], gkB, mods, ("mods",), 5, False)
        SCALE = 128.0 ** -0.5
        NK = 66
        PS_S = (0, 1, 2)
        for kvh in range(2):
            if kvh == 1:
                mods_layer1()
            for part in range(4):
                if FU:
                    P.load(KT[:, part * 2112:(part + 1) * 2112], k_all[part * 256 + kvh * 128: part * 256 + (kvh + 1) * 128, :],
                           ("KV", "k", part), after=[cck])
                else:
                    P.load(KT[:, part * 2112:(part + 1) * 2112], d_k[kvh, :, part * 2112:(part + 1) * 2112], ("KV", "k", part))
            if FU:
                vv = v_all[:, kvh * 128:(kvh + 1) * 128].rearrange("(k p) d -> p k d", p=128)
            else:
                vv = d_v[kvh].rearrange("(k p) d -> p k d", p=128)
            for part in range(6):
                P.load(V[:, part * 11:(part + 1) * 11, :], vv[:, part * 11:(part + 1) * 11, :], ("KV", "v", part),
                       after=([ccv] if FU else []))
            for qg in range(4):
                qb = P.nxt("QTg", 2)
                for hl in range(4):
                    P.load(QTg[qb][:, hl, :], d_q[4 * kvh + hl, :, qg * 512:(qg + 1) * 512], ("QT", qb, hl),
                           after=(qstores if FU else []))
                for hl in range(4):
                    hh = 4 * kvh + hl
                    bo, bz = 4 + (hl % 2), 6 + (hl % 2)
                    NP_ = NK // 2

                    def s_pair(j, hl=hl, qb=qb):
                        sp_ = j % 2
                        pb_ = j % 3
                        for t_ in range(2):
                            kt = 2 * j + t_
                            part = (kt * 128) // 2112
                            part2 = (kt * 128 + 127) // 2112
                            S.add("pe", lambda e, kt=kt, t_=t_: e.matmul(psb[2 * sp_ + t_], lhsT=KT[:, kt * 128:(kt + 1) * 128],
                                                                       rhs=QTg[qb][:, hl, :], start=True, stop=True),
                                  reads=[("KV", "k", part), ("KV", "k", part2), ("QT", qb, hl)], writes=[("ps", 2 * sp_ + t_)])
                        S.add("act", lambda e: e.activation(out=Pb[pb_], in_=PSALL[:, sp_ * 1024:(sp_ + 1) * 1024], func=AF.Exp,
                                                            bias=negc[:, 0:1], scale=SCALE),
                              reads=[("ps", 2 * sp_), ("ps", 2 * sp_ + 1), ("negc",)], writes=[("Pb", pb_)])
                        S.add("dve", lambda e: e.tensor_tensor(out=Pz[pb_], in0=Pb[pb_][:, 0:512], in1=Pb[pb_][:, 512:1024], op=ALU.add),
                              reads=[("Pb", pb_)], writes=[("Pz", pb_)])

                    def pv_pair(j, bo=bo, bz=bz):
                        sp_ = j % 2
                        pb_ = j % 3
                        for t_ in range(2):
                            kt = 2 * j + t_
                            S.add("pe", lambda e, kt=kt, t_=t_: e.matmul(psb[bo], lhsT=V[:, kt, :], rhs=Pb[pb_][:, t_ * 512:(t_ + 1) * 512],
                                                                       start=(kt == 0), stop=(kt == NK - 1)),
                                  reads=[("KV", "v", kt // 11), ("Pb", pb_)], writes=[("ps", bo)])
                        S.add("pe", lambda e: e.matmul(psb[bz], lhsT=ones[:], rhs=Pz[pb_], start=(j == 0), stop=(j == NP_ - 1)),
                              reads=[("ones",), ("Pz", pb_)], writes=[("ps", bz)])
                    s_pair(0)
                    s_pair(1)
                    for j in range(NP_):
                        if j + 2 < NP_:
                            s_pair(j + 2)
                        pv_pair(j)
                    ti = P.nxt("tmp", 2)
                    S.add("dve", lambda e, ti=ti, bz=bz: e.reciprocal(out=tmp[ti][:], in_=psb[bz]), reads=[("ps", bz)], writes=[("tmp", ti)])
                    S.add("dve", lambda e, ti=ti, bo=bo, hh=hh, qg=qg: e.tensor_tensor(out=OTall[:, hh, qg * 512:(qg + 1) * 512], in0=psb[bo], in1=tmp[ti][:], op=ALU.mult),
                          reads=[("ps", bo), ("tmp", ti)], writes=[("OTall", hh, qg)])
        S.alias("yw", ["QT", "Pb", "Pz", "h1", "h2", "ysb", "diff", "ym", "pscr", "wpool"])
        woget = wstream([d_wo[oc] for qg in range(4) for oc in range(8)], 3)
        for qg in range(4):
            for oc in range(8):
                wi = woget(qg * 8 + oc)
                b = P.nxt("psmain", 4)
                for hh in range(8):
                    S.add("pe", lambda e, b=b, wi=wi, hh=hh, qg=qg: e.matmul(
                        psb[b], lhsT=WB[wi][:, hh * 128:(hh + 1) * 128],
                        rhs=OTall[:, hh, qg * 512:(qg + 1) * 512], start=(hh == 0), stop=(hh == 7)),
                        reads=[("wb", wi), ("OTall", hh, qg)], writes=[("ps", b)])
                S.add("dve", lambda e, b=b, oc=oc: e.tensor_copy(out=yw[:, oc, :], in_=psb[b]), reads=[("ps", b)], writes=[("yw", oc)])
            xfn, xkey = xmain(qg)
            yfn = lambda kc: yw[:, kc, :]
            ykey = lambda kc: ("yw", kc)
            if DBG and qg == 0:
                for kc in range(8):
                    P.store(g_ot[:, kc, :], OTall[:, kc, 0:512], ("OTall", kc, 0))
                    P.store(g_yw[:, kc, :], yw[:, kc, :], ("yw", kc))
                P.store(g_m, mods[:].rearrange("p o v -> p (o v)"), ("mods",))
                P.store(g_nc, negc[:], ("negc",))
            r = rms_rstd(yfn, ykey, 512)
            resid_add(xfn, xkey, yfn, ykey, 512, r, Gp1, ("Gp1",), 0)
            if DBG and qg == 0:
                for kc in range(8):
                    P.store(g_xm[:, kc, :], X[:, kc, 0:512], ("X", kc, 0))
        def lat_group(g, uc):
            xfn, xkey = xmain(g)
            return (xfn, xkey, 512, 0, uc)
        mlp([lat_group(0, 0), lat_group(1, 512)], d_winB, d_woutB)
        mlp([lat_group(2, 0), lat_group(3, 512)], d_winB, d_woutB)
        yv = o_y.rearrange("(c p) t -> p c t", p=128)
        for kc in range(8):
            S.add("sp", lambda e, kc=kc: e.dma_start(out=yv[:, kc, :], in_=X[:, kc, XOFF:XOFF + T]),
                  reads=[("X", kc, XOFF + g * 512) for g in range(4)], dma=True)
            P.finals.append(S.all_ops[-1])

    return P


def build(mode):
    try:
        P = _build(mode)
    except _Cut as c:
        P = c.args[0]
    P.S.emit_all(P.nc, P.st, final_waits=P.finals)
    P.st.close()
    return P.nc


def chunk_in(W, ncols):
    K, N = W.shape
    a = W.reshape(K // 128, 128, N // ncols, ncols)
    return np.ascontiguousarray(a.transpose(2, 1, 0, 3).reshape(N // ncols, 128, (K // 128) * ncols))


def vec8(v):
    return np.ascontiguousarray(v.reshape(-1, 128).T)


_NC_CACHE = {}
FUSED = os.environ.get("KFUSED", "0") == "1"
_DBG = {}


def get_nc(mode):
    if mode not in _NC_CACHE:
        _NC_CACHE[mode] = build(mode)
    return _NC_CACHE[mode]


def kernel(x, c, ctx, c_ctx, w_ada, b_ada, g_mix_pre, g_mix_post, g_mlp_pre, g_mlp_post,
           w_pool, pool_scale, w_qkv, g_q, g_k, w_o, w_mlp_in, w_mlp_out):
    f = lambda a: np.asarray(a, dtype=np.float32)
    x, c, ctx, c_ctx, w_ada, b_ada = map(f, (x, c, ctx, c_ctx, w_ada, b_ada))
    g_mix_pre, g_mix_post, g_mlp_pre, g_mlp_post = map(f, (g_mix_pre, g_mix_post, g_mlp_pre, g_mlp_post))
    w_pool, pool_scale, w_qkv, g_q, g_k, w_o, w_mlp_in, w_mlp_out = map(f, (w_pool, pool_scale, w_qkv, g_q, g_k, w_o, w_mlp_in, w_mlp_out))
    B, L, D = x.shape
    C = ctx.shape[1]

    def common(layer):
        wout = w_mlp_out[layer]
        a = wout.reshape(4, 8, 128, 8, 128)
        wout_c = np.ascontiguousarray(a.transpose(3, 0, 2, 1, 4).reshape(32, 128, 1024))
        return {
            "wada": chunk_in(w_ada[layer], 128),
            "bada": vec8(b_ada[layer]),
            "gains": np.ascontiguousarray(np.concatenate([vec8(g_mix_pre[layer]), vec8(g_mix_post[layer]),
                                                          vec8(g_mlp_pre[layer]), vec8(g_mlp_post[layer])], axis=1)),
            "w_in": chunk_in(w_mlp_in[layer], 128),
            "w_out": wout_c,
        }
    cmA, cmB = common(0), common(1)
    rot = np.zeros((128, 128), np.float32)
    for m in list(range(0, 32)) + list(range(64, 96)):
        rot[m + 32, m] = -1.0
        rot[m, m + 32] = 1.0
    inv_freq = np.power(np.float32(10000.0), -np.arange(0, 64, 2, dtype=np.float32) / np.float32(64))
    wpool_c = np.ascontiguousarray(w_pool[0].reshape(2, 2, 2, 128, 256).transpose(0, 3, 1, 2, 4).reshape(2, 128, 1024))
    wqkv_c = chunk_in(w_qkv[0], 128)
    wada1_c = chunk_in(w_ada[1][:, :2048], 128)
    mapsA = []
    for core in range(8):
        b, j = core // 4, core % 4
        t0 = j * T
        xp = np.zeros((T + 2 * HO, D), np.float32)
        lo, hi = max(t0 - HO, 0), min(t0 + T + HO, L)
        xp[lo - (t0 - HO):hi - (t0 - HO)] = x[b, lo:hi]
        c0 = j * TC
        cp = np.zeros((TC + 2 * HO, D), np.float32)
        lo, hi = max(c0 - HO, 0), min(c0 + TC + HO, C)
        cp[lo - (c0 - HO):hi - (c0 - HO)] = ctx[b, lo:hi]
        cond = np.stack([vec8(c[b]), vec8(c_ctx)], axis=-1).reshape(128, 16)
        edge = np.zeros((4, 16), np.float32)
        for g, w in enumerate((2, 4, 8, 16)):
            for i in range(8):
                tl = i
                cl = (tl + w - w // 2) - max(tl - w // 2, 0) if j == 0 else w
                tr = 8 - i
                cr = min(w - w // 2, tr) + w // 2 if j == 3 else w
                edge[g, i] = 1.0 / cl
                edge[g, 8 + i] = 1.0 / cr
        hmask = np.array([0.0 if j == 0 else 1.0, 0.0 if j == 3 else 1.0], np.float32)
        tt = np.arange(t0, t0 + T)
        row = (tt // 64).astype(np.float32); col = (tt % 64).astype(np.float32)
        ang = np.concatenate([row[None, :] * inv_freq[:, None]] * 2 + [col[None, :] * inv_freq[:, None]] * 2, axis=0)
        m = dict(cmA)
        m.update({
            "cond": np.ascontiguousarray(cond), "xT": np.ascontiguousarray(xp.T), "ctxT": np.ascontiguousarray(cp.T),
            "wada1": wada1_c, "bada1": vec8(b_ada[1]), "g1pre": vec8(g_mix_pre[1]),
            "w_pool": wpool_c, "pscale": vec8(pool_scale[0]),
            "edge": np.ascontiguousarray(np.broadcast_to(edge.reshape(1, 64), (128, 64))),
            "hmask": np.ascontiguousarray(np.broadcast_to(hmask.reshape(1, 2), (128, 2))),
            "w_qkv": wqkv_c, "gqk": np.ascontiguousarray(np.stack([g_q[0], g_k[0]], axis=1)),
            "cosT": np.ascontiguousarray(np.cos(ang).astype(np.float32)), "sinT": np.ascontiguousarray(np.sin(ang).astype(np.float32)),
            "rot": rot,
        })
        mapsA.append(m)
    if FUSED:
        wo_c = chunk_in(w_o[0], 128)
        gqkrow = np.ascontiguousarray(np.concatenate([g_q[0], g_k[0]]).reshape(1, 256))
        mapsF = []
        for core in range(8):
            m = dict(mapsA[core])
            m.pop("wada1")
            m.update({"wadaB": cmB["wada"], "gainsB": cmB["gains"], "w_inB": cmB["w_in"], "w_outB": cmB["w_out"],
                      "w_o": wo_c, "gqkrow": gqkrow})
            mapsF.append(m)
        resF = run_bass_kernel_spmd(get_nc("F"), mapsF, core_ids=list(range(8))).results
        out = np.empty((B, L, D), np.float32)
        for core in range(8):
            b, j = core // 4, core % 4
            out[b, j * T:(j + 1) * T] = np.asarray(resF[core]["yT"]).T
        return out
    resA = run_bass_kernel_spmd(get_nc("A"), mapsA, core_ids=list(range(8))).results
    _DBG["resA"] = resA
    wo_c = chunk_in(w_o[0], 128)
    mapsB = []
    kall, vall = {}, {}
    for b in range(B):
        ks = [np.asarray(resA[4 * b + j]["kT"]) for j in range(4)]
        vs = [np.asarray(resA[4 * b + j]["v"]) for j in range(4)]
        kall[b] = np.ascontiguousarray(np.concatenate([k[:, :, T:] for k in ks] + [k[:, :, :T] for k in ks], axis=2))
        vcat = np.concatenate([v[T:] for v in vs] + [v[:T] for v in vs], axis=0)
        vall[b] = np.ascontiguousarray(vcat.reshape(8448, 2, 128).transpose(1, 0, 2))
    for core in range(8):
        b = core // 4
        cond = np.stack([vec8(c[b]), vec8(c_ctx)], axis=-1).reshape(128, 16)
        m = dict(cmB)
        m.update({
            "cond": np.ascontiguousarray(cond), "x1T": np.asarray(resA[core]["x1T"]), "qT": np.asarray(resA[core]["qT"]),
            "kTall": kall[b], "vall": vall[b], "w_o": wo_c,
            "gqkrow": np.ascontiguousarray(np.concatenate([g_q[0], g_k[0]]).reshape(1, 256)),
        })
        mapsB.append(m)
    resB = run_bass_kernel_spmd(get_nc("B"), mapsB, core_ids=list(range(8))).results
    out = np.empty((B, L, D), np.float32)
    for core in range(8):
        b, j = core // 4, core % 4
        out[b, j * T:(j + 1) * T] = np.asarray(resB[core]["yT"]).T
    return out
```

```python
import contextlib
import os
import numpy as np
import ml_dtypes
import concourse.bass as bass
import concourse.mybir as mybir
from concourse.bass_utils import run_bass_kernel_spmd

F32 = mybir.dt.float32
BF16 = mybir.dt.bfloat16
AF = mybir.ActivationFunctionType
ALU = mybir.AluOpType
NPBF = ml_dtypes.bfloat16

ENGS = ("pe", "act", "dve", "pool", "sp")
T = 2048
TC = 64
HO = 8
EPS = 1e-6
NWB = 8


class Op:
    __slots__ = ("eng", "emit", "deps", "dma", "sig", "cnt", "sem", "idx", "nosig")

    def __init__(self, eng, emit, dma):
        self.eng, self.emit, self.dma = eng, emit, dma
        self.nosig = False
        self.deps, self.sig, self.cnt, self.sem = [], False, 0, None


class Sched:
    def __init__(self, n_dma_sems=8):
        self.ops = {e: [] for e in ENGS}
        self.all_ops = []
        self.last_w, self.readers, self.alias_deps = {}, {}, {}
        self.n_dma_sems = n_dma_sems
        self.dma_ring = {e: [] for e in ENGS}

    def alias(self, new_buf, old_bufs):
        s = self.alias_deps.setdefault(new_buf, [])
        olds, seen = set(old_bufs), set(id(o) for o in s)
        for k, w in self.last_w.items():
            if k[0] in olds and id(w) not in seen:
                s.append(w); seen.add(id(w))
        for k, rs in self.readers.items():
            if k[0] in olds:
                for r in rs:
                    if id(r) not in seen:
                        s.append(r); seen.add(id(r))

    def add(self, eng, emit, reads=(), writes=(), dma=False, nosig=False, after=()):
        op = Op(eng, emit, dma)
        op.nosig = nosig
        deps = {id(o): o for o in after}
        for k in reads:
            w = self.last_w.get(k)
            if w is not None:
                deps[id(w)] = w
            self.readers.setdefault(k, []).append(op)
            for o in self.alias_deps.get(k[0], ()):
                deps[id(o)] = o
        for k in writes:
            w = self.last_w.get(k)
            if w is not None:
                deps[id(w)] = w
            for r in self.readers.get(k, ()):
                deps[id(r)] = r
            self.readers[k] = []
            self.last_w[k] = op
            for o in self.alias_deps.get(k[0], ()):
                deps[id(o)] = o
        deps.pop(id(op), None)
        if dma:
            ring = self.dma_ring[eng]
            if len(ring) >= self.n_dma_sems:
                prev = ring[len(ring) - self.n_dma_sems]
                deps[id(prev)] = prev
            ring.append(op)
        best, keep = {}, []
        for d in deps.values():
            if d.dma:
                keep.append(d)
            elif d.eng not in best or d.idx > best[d.eng].idx:
                best[d.eng] = d
        op.deps = keep + list(best.values())
        op.idx = len(self.all_ops)
        self.ops[eng].append(op)
        self.all_ops.append(op)
        return op

    def emit_all(self, nc, st, final_waits=()):
        SAME_OK = ("pe", "act", "dve") if os.environ.get("KSAME", "0") == "1" else ("pe",)

        def skip(d, op):
            return d.eng == op.eng and d.eng in SAME_OK and not d.dma and not op.dma
        for op in self.all_ops:
            for d in op.deps:
                if not skip(d, op):
                    d.sig = True
        for op in final_waits:
            op.sig = True
        for op in self.all_ops:
            assert not (op.sig and op.nosig), "fp32 matmul must not carry a semaphore increment"
        esem = {e: st.enter_context(nc.semaphore(f"s_{e}")) for e in ENGS}
        dsem = {e: [st.enter_context(nc.semaphore(f"d_{e}{i}")) for i in range(self.n_dma_sems)]
                for e in ENGS if self.dma_ring[e]}
        for e in ENGS:
            k, ecnt, dcnt = 0, 0, [0] * self.n_dma_sems
            for op in self.ops[e]:
                if op.dma:
                    i = k % self.n_dma_sems
                    k += 1
                    dcnt[i] += 16
                    op.sem, op.cnt, op.sig = dsem[e][i], dcnt[i], True
                elif op.sig:
                    ecnt += 1
                    op.sem, op.cnt = esem[e], ecnt
        block = st.enter_context(nc.Block())
        engmap = {"pe": block.tensor, "act": block.scalar, "dve": block.vector,
                  "pool": block.gpsimd, "sp": block.sync}

        def make(e):
            def body(eng):
                waited = {}
                for op in self.ops[e]:
                    need = {}
                    for d in op.deps:
                        if skip(d, op):
                            continue
                        key = id(d.sem)
                        if d.cnt > waited.get(key, 0) and d.cnt > need.get(key, (0, None))[0]:
                            need[key] = (d.cnt, d.sem)
                    for key, (cnt, sem) in need.items():
                        eng.wait_ge(sem, cnt)
                        waited[key] = cnt
                    ins = op.emit(eng)
                    if op.sig:
                        ins.then_inc(op.sem, 16 if op.dma else 1)
                if e == "sp":
                    for op in final_waits:
                        eng.wait_ge(op.sem, op.cnt)
            return body

        for e in ENGS:
            if self.ops[e] or e == "sp":
                engmap[e](make(e))


class Prog:
    def __init__(self, mode):
        self.mode = mode
        self.nc = bass.Bass("TRN2", target_bir_lowering=False)
        self.S = Sched(int(os.environ.get("KSEMS", "8")))
        self.st = contextlib.ExitStack()
        self.finals = []
        self.rr = {}

    def din(self, name, shape, dt=F32):
        return self.nc.dram_tensor(name, list(shape), dt, kind="ExternalInput").ap()

    def dout(self, name, shape, dt=F32):
        return self.nc.dram_tensor(name, list(shape), dt, kind="ExternalOutput").ap()

    def sb(self, name, shape, dt=F32):
        return self.st.enter_context(self.nc.sbuf_tensor("sb_" + name, list(shape), dt))

    def ps(self, name):
        return self.st.enter_context(self.nc.psum_tensor(name, [128, 512], F32))

    def nxt(self, name, n):
        i = self.rr.get(name, 0)
        self.rr[name] = i + 1
        return i % n

    def ew(self, name):
        return ("dve", "pool")[self.nxt(name, 2)]

    def load(self, dst, src, wkey, eng="sp", after=()):
        return self.S.add(eng, lambda e: e.dma_start(out=dst, in_=src), writes=[wkey], dma=True, after=after)

    def dint(self, name, shape, dt=F32):
        return self.nc.dram_tensor(name, list(shape), dt, kind="Internal").ap()

    def store(self, dst, src, rkey, eng="sp"):
        op = self.S.add(eng, lambda e: e.dma_start(out=dst, in_=src), reads=[rkey], dma=True)
        self.finals.append(op)
        return op


class _Cut(Exception):
    pass


def _build(mode):
    P = Prog(mode)
    CUT = int(os.environ.get("KCUT", "99"))
    nc, S = P.nc, P.S
    A = mode in ("A", "F")
    Bm = mode in ("B", "F")
    FU = mode == "F"
    DBG = os.environ.get("KDBG") == "1" and not FU
    d_cond = P.din("cond", [128, 16])
    d_wada = P.din("wada", [48, 128, 1024])
    d_bada = P.din("bada", [128, 48])
    d_gains = P.din("gains", [128, 32])
    d_win = P.din("w_in", [32, 128, 1024])
    d_wout = P.din("w_out", [32, 128, 1024])
    if A:
        d_x = P.din("xT", [1024, T + 2 * HO])
        d_ctx = P.din("ctxT", [1024, TC + 2 * HO])
        if not FU:
            d_wada1 = P.din("wada1", [16, 128, 1024])
        d_bada1 = P.din("bada1", [128, 48])
        d_g1pre = P.din("g1pre", [128, 8])
        d_wpool = P.din("w_pool", [2, 128, 1024])
        d_pscale = P.din("pscale", [128, 8])
        d_edge = P.din("edge", [128, 64])
        d_hmask = P.din("hmask", [128, 2])
        d_wqkv = P.din("w_qkv", [12, 128, 1024])
        d_gqk = P.din("gqk", [128, 2])
        d_cos = P.din("cosT", [128, T])
        d_sin = P.din("sinT", [128, T])
        d_rot = P.din("rot", [128, 128])
        if FU:
            o_q = P.dint("q_d", [8, 128, T], BF16)
            k_loc = P.dint("k_loc", [256, T + TC], BF16)
            o_k = k_loc.rearrange("(h d) t -> h d t", h=2)
            o_v = P.dint("v_loc", [T + TC, 256], BF16)
            k_all = P.dint("k_all", [4 * 256, T + TC], BF16)
            v_all = P.dint("v_all", [4 * (T + TC), 256], BF16)
        else:
            o_x1 = P.dout("x1T", [1024, T])
            o_q = P.dout("qT", [8, 128, T], BF16)
            o_k = P.dout("kT", [2, 128, T + TC], BF16)
            o_v = P.dout("v", [T + TC, 256], BF16)
        if DBG:
            g_h2 = P.dout("dbg_h2", [128, 8, 512], BF16)
            g_u = P.dout("dbg_u", [128, 32, 512], BF16)
            g_y = P.dout("dbg_ysb", [128, 8, 512])
            g_w = P.dout("dbg_wb", [128, 1024], BF16)
            g_m = P.dout("dbg_mods", [128, 96])
            g_G = P.dout("dbg_G", [128, 64])
            g_b = P.dout("dbg_bada", [128, 48])
    if Bm:
        if FU:
            d_q = o_q
            d_wadaB = P.din("wadaB", [48, 128, 1024])
            d_gainsB = P.din("gainsB", [128, 32])
            d_winB = P.din("w_inB", [32, 128, 1024])
            d_woutB = P.din("w_outB", [32, 128, 1024])
        else:
            d_x = P.din("x1T", [1024, T])
            d_q = P.din("qT", [8, 128, T], BF16)
            d_k = P.din("kTall", [2, 128, 8448], BF16)
            d_v = P.din("vall", [2, 8448, 128], BF16)
            d_wadaB, d_winB, d_woutB = d_wada, d_win, d_wout
        d_wo = P.din("w_o", [8, 128, 1024])
        d_gqkrow = P.din("gqkrow", [1, 256])
        o_y = P.dout("yT", [1024, T])
        if DBG:
            g_ot = P.dout("dbg_ot", [128, 8, 512], BF16)
            g_yw = P.dout("dbg_yw", [128, 8, 512])
            g_xm = P.dout("dbg_xm", [128, 8, 512])
            g_m = P.dout("dbg_mods", [128, 96])
            g_nc = P.dout("dbg_negc", [128, 1])

    XW = T + 2 * HO
    XOFF = HO if A else 0
    X = P.sb("X", [128, 8, XW if A else T])
    BIG = P.sb("BIG", [128, 17408])
    MID = P.sb("MID", [128, 8704])
    STG = [P.sb(f"stg{i}", [128, 1024]) for i in range(2)]
    WB = [P.sb(f"wb{i}", [128, 1024], BF16) for i in range(NWB)]
    rstd = [P.sb(f"rstd{i}", [128, 512]) for i in range(2)]
    tmp = [P.sb(f"tmp{i}", [128, 512]) for i in range(2)]
    sq = [P.sb(f"sq{i}", [128, 512], BF16) for i in range(2)]
    ones = P.sb("ones", [128, 128], BF16)
    cond = P.sb("cond", [128, 8, 2])
    condb = P.sb("condb", [128, 8, 2], BF16)
    mods = P.sb("mods", [128, 48, 2])
    bada = P.sb("bada", [128, 48])
    gains = P.sb("gains", [128, 4, 8])
    G1 = P.sb("G1", [128, 8, 2]); Gp1 = P.sb("Gp1", [128, 8, 2])
    G2 = P.sb("G2", [128, 8, 2]); Gp2 = P.sb("Gp2", [128, 8, 2])
    PSALL = P.st.enter_context(nc.psum_tensor("psall", [128, 4096], F32))
    psb = [PSALL[:, i * 512:(i + 1) * 512] for i in range(8)]
    u = BIG[:].bitcast(BF16).rearrange("p (c t) -> p c t", c=32)
    h2 = MID[:].bitcast(BF16)[:, 0:8 * 1088].rearrange("p (c t) -> p c t", c=8)
    ysb = MID[:, 0:8 * 1088].rearrange("p (c t) -> p c t", c=8)
    if A:
        XC = P.sb("XC", [128, 8, TC + 2 * HO])
        pscale = P.sb("pscale", [128, 8])
        edge = P.sb("edge", [128, 4, 16])
        hmask = P.sb("hmask", [128, 2])
        gqk = P.sb("gqk", [128, 2])
        mods1 = P.sb("mods1", [128, 16, 2])
        bada1 = P.sb("bada1", [128, 48])
        g1pre = P.sb("g1pre", [128, 8])
        G1b = P.sb("G1b", [128, 8, 2])
        hbuf = BIG[:, 0:8 * XW].rearrange("p (c t) -> p c t", c=8)
        diff = MID[:].bitcast(BF16)[:, 0:2048].rearrange("p (c t) -> p c t", c=8)
        ym = MID[:, 1024:1024 + 2048].rearrange("p (c t) -> p c t", c=8)
        pscr = [MID[:, 3072 + i * 272: 3072 + (i + 1) * 272] for i in range(4)]
        wpool = MID[:, 4160:4160 + 1024].bitcast(BF16)
        HC = BIG[:, 8 * XW: 8 * XW + 8 * (TC + 2 * HO)].rearrange("p (c t) -> p c t", c=8)
        ostg = [BIG[:, 2 * T + 1024 + i * 256: 2 * T + 1024 + (i + 1) * 256].bitcast(BF16) for i in range(2)]
        rot = BIG[:, 2 * T + 1536: 2 * T + 1536 + 128]
    if Bm:
        gainsB = P.sb("gainsB", [128, 4, 8]) if FU else gains
        gqkrow = P.sb("gqkrow", [1, 256])
        negc = P.sb("negc", [128, 1])
        c1 = P.sb("c1", [1, 4])
        ones32 = P.sb("ones32", [1, 128])
        BB = BIG[:].bitcast(BF16)
        KT = BB[:, 0:8448]
        V = BB[:, 8448:2 * 8448].rearrange("p (k d) -> p k d", k=66)
        OTall = BB[:, 2 * 8448:2 * 8448 + 8 * T].rearrange("p (h t) -> p h t", h=8)
        MB = MID[:].bitcast(BF16)
        QTg = [MB[:, i * 2048:(i + 1) * 2048].rearrange("p (h t) -> p h t", h=4) for i in range(2)]
        yw = MID[:, 0:4096].rearrange("p (c t) -> p c t", c=8)
        Pb = [MB[:, 4096 + i * 1024: 4096 + (i + 1) * 1024] for i in range(3)]
        Pz = [MB[:, 7168 + i * 512: 7168 + (i + 1) * 512] for i in range(3)]

    xv = d_x.rearrange("(c p) t -> p c t", p=128)
    for kc in range(8):
        P.load(X[:, kc, 0:(XW if A else T)], xv[:, kc, :], ("X", kc))
    if A:
        cv = d_ctx.rearrange("(c p) t -> p c t", p=128)
        P.load(XC[:], cv, ("XC",))
    S.add("pool", lambda e: e.memset(ones[:], 1.0), writes=[("ones",)])
    P.load(cond[:].rearrange("p c v -> p (c v)"), d_cond, ("cond",))
    P.load(bada[:], d_bada, ("bada",))
    P.load(gains[:].rearrange("p a c -> p (a c)"), d_gains, ("gains",))
    S.add("act", lambda e: e.activation(out=condb[:], in_=cond[:], func=AF.Silu),
          reads=[("cond",)], writes=[("condb",)])

    def stage(src):
        i = P.nxt("stg", 2)
        P.load(STG[i][:], src, ("stg", i))
        return i

    def cast(si):
        i = P.nxt("wb", NWB)
        eng = ("act", "dve")[P.nxt("casteng", 2)]
        if eng == "act":
            S.add("act", lambda e: e.copy(out=WB[i][:], in_=STG[si][:]), reads=[("stg", si)], writes=[("wb", i)])
        else:
            S.add("dve", lambda e: e.tensor_copy(out=WB[i][:], in_=STG[si][:]), reads=[("stg", si)], writes=[("wb", i)])
        return i

    def wstream(chunks, depth):
        st_ = {"next": 0, "wi": {}}

        def get(i):
            while st_["next"] <= min(i + depth, len(chunks) - 1):
                k = st_["next"]
                st_["wi"][k] = cast(stage(chunks[k]))
                st_["next"] += 1
            return st_["wi"][i]
        return get

    PS_MODS = 7

    def pe_marker(reads, writes):
        S.add("pe", lambda e: e.matmul(psb[PS_MODS][:, 510:512], lhsT=ones[:], rhs=ones[:, 0:2], start=True, stop=True),
              reads=list(reads) + [("ones",)], writes=list(writes) + [("ps_mark",)])

    def ada_mm(dw, chunks, oc0):
        pm = psb[PS_MODS]
        wget = wstream([dw[ci] for ci in chunks], 3)
        for idx, ci in enumerate(chunks):
            wi = wget(idx)
            oc = ci - oc0
            for kc in range(8):
                S.add("pe", lambda e, wi=wi, kc=kc, oc=oc: e.matmul(
                    pm[:, 2 * oc:2 * oc + 2], lhsT=WB[wi][:, kc * 128:(kc + 1) * 128],
                    rhs=condb[:, kc, :], start=(kc == 0), stop=(kc == 7)),
                    reads=[("wb", wi), ("condb",)], writes=[("ps", PS_MODS)])

    def ada_evac(lo, hi, mods_t, mkey, bada_t, bkey, oc0):
        pm = psb[PS_MODS]
        a, b_ = lo - oc0, hi - oc0
        for v in range(2):
            S.add("dve", lambda e, v=v: e.tensor_tensor(
                out=mods_t[:, a:b_, v], in0=pm[:, 2 * a:2 * b_].rearrange("p (o v) -> p o v", v=2)[:, :, v],
                in1=bada_t[:, lo:hi], op=ALU.add),
                reads=[("ps", PS_MODS), bkey], writes=[mkey])

    def ada(dw, chunks, mods_t, mkey, bada_t, bkey, oc0):
        ada_mm(dw, chunks, oc0)
        ada_evac(chunks[0], chunks[-1] + 1, mods_t, mkey, bada_t, bkey, oc0)

    def combine(out_t, okey, gain_ap, gkey, mods_t, mkey, mi, plus_one):
        for v in range(2):
            if plus_one:
                S.add("dve", lambda e, v=v: e.scalar_tensor_tensor(
                    out=out_t[:, :, v], in0=mods_t[:, mi * 8:(mi + 1) * 8, v], scalar=1.0, in1=gain_ap,
                    op0=ALU.add, op1=ALU.mult), reads=[mkey, gkey], writes=[okey])
            else:
                S.add("dve", lambda e, v=v: e.tensor_tensor(
                    out=out_t[:, :, v], in0=mods_t[:, mi * 8:(mi + 1) * 8, v], in1=gain_ap, op=ALU.mult),
                    reads=[mkey, gkey], writes=[okey])

    PS_STAT = 6

    def rms_rstd(src_fn, src_keys, n, scale_ap=None):
        pst = psb[PS_STAT]
        for kc in range(8):
            qi = P.nxt("sq", 2)
            S.add("act", lambda e, kc=kc, qi=qi: e.activation(out=sq[qi][:, :n], in_=src_fn(kc), func=AF.Square),
                  reads=[src_keys(kc)], writes=[("sq", qi)])
            S.add("pe", lambda e, kc=kc, qi=qi: e.matmul(pst[:, :n], lhsT=ones[:], rhs=sq[qi][:, :n],
                                                         start=(kc == 0), stop=(kc == 7)),
                  reads=[("sq", qi), ("ones",)], writes=[("ps", PS_STAT)])
        r = P.nxt("rstd", 2)
        S.add("act", lambda e: e.activation(out=rstd[r][:, :n], in_=pst[:, :n], func=AF.Ln, bias=EPS, scale=1.0 / 1024),
              reads=[("ps", PS_STAT)], writes=[("rstd", r)])
        S.add("act", lambda e: e.activation(out=rstd[r][:, :n], in_=rstd[r][:, :n], func=AF.Exp, scale=-0.5),
              reads=[("rstd", r)], writes=[("rstd", r)])
        return r

    def modulate(src_fn, src_keys, n, r, Gt, gk, sht, shk, shi, v, dst_fn, dst_keys):
        for kc in range(8):
            ti = P.nxt("tmp", 2)
            S.add("dve", lambda e, kc=kc, ti=ti: e.tensor_tensor(out=tmp[ti][:, :n], in0=src_fn(kc), in1=rstd[r][:, :n], op=ALU.mult),
                  reads=[src_keys(kc), ("rstd", r)], writes=[("tmp", ti)])
            S.add("act", lambda e, kc=kc, ti=ti: e.activation(out=dst_fn(kc), in_=tmp[ti][:, :n], func=AF.Identity,
                                                              bias=sht[:, shi * 8 + kc, v:v + 1], scale=Gt[:, kc, v:v + 1]),
                  reads=[("tmp", ti), gk, shk], writes=[dst_keys(kc)])

    def resid_add(xfn, xkeys, yfn, ykeys, n, r, Gpt, gpk, v):
        for kc in range(8):
            ti = P.nxt("tmp", 2)
            S.add("dve", lambda e, kc=kc, ti=ti: e.scalar_tensor_tensor(
                out=tmp[ti][:, :n], in0=yfn(kc), scalar=Gpt[:, kc, v:v + 1], in1=rstd[r][:, :n], op0=ALU.mult, op1=ALU.mult),
                reads=[ykeys(kc), gpk, ("rstd", r)], writes=[("tmp", ti)])
            S.add("pool", lambda e, kc=kc, ti=ti: e.tensor_tensor(out=xfn(kc), in0=xfn(kc), in1=tmp[ti][:, :n], op=ALU.add),
                  reads=[("tmp", ti), xkeys(kc)], writes=[xkeys(kc)])

    dbg_state = {"n": 0}

    def mlp(groups, d_win=d_win, d_wout=d_wout):
        dbg = A and DBG and dbg_state["n"] == 0
        dbg_state["n"] += 1
        S.alias("h2", ["ysb", "diff", "ym", "pscr", "QT", "Pb", "Pz", "wpool", "yw"])
        S.alias("u", ["hbuf", "HC", "KV", "OTall", "rope"])
        for (xfn, xkey, n, v, uc) in groups:
            r = rms_rstd(xfn, xkey, n)
            modulate(xfn, xkey, n, r, G2, ("G2",), mods, ("mods",), 3, v,
                     lambda kc, uc=uc, n=n: h2[:, kc, uc:uc + n], lambda kc, uc=uc: ("h2", kc, uc))
        if dbg:
            P.store(g_m, mods[:].rearrange("p o v -> p (o v)"), ("mods",))
            P.store(g_b, bada[:], ("bada",))
            for i_, (Gt_, gk_) in enumerate(((G1, ("G1",)), (Gp1, ("Gp1",)), (G2, ("G2",)), (Gp2, ("Gp2",)))):
                P.store(g_G[:, i_ * 16:(i_ + 1) * 16], Gt_[:].rearrange("p c v -> p (c v)"), gk_)
            for kc in range(8):
                P.store(g_h2[:, kc, :], h2[:, kc, 0:512], ("h2", kc, 0))
        wget = wstream([d_win[i] for i in range(32)] + [d_wout[i] for i in range(32)], 4)
        for oc in range(32):
            wi = wget(oc)
            if dbg and oc == 5:
                P.store(g_w, WB[wi][:], ("wb", wi))
            for (xfn, xkey, n, v, uc) in groups:
                b = P.nxt("psmain", 4)
                for kc in range(8):
                    S.add("pe", lambda e, b=b, kc=kc, wi=wi, uc=uc, n=n: e.matmul(
                        psb[b][:, :n], lhsT=WB[wi][:, kc * 128:(kc + 1) * 128],
                        rhs=h2[:, kc, uc:uc + n], start=(kc == 0), stop=(kc == 7)),
                        reads=[("wb", wi), ("h2", kc, uc)], writes=[("ps", b)])
                ti = P.nxt("tmp", 2)
                S.add("act", lambda e, b=b, ti=ti, n=n: e.activation(out=tmp[ti][:, :n], in_=psb[b][:, :n], func=AF.Relu),
                      reads=[("ps", b)], writes=[("tmp", ti)])
                S.add("dve", lambda e, ti=ti, oc=oc, uc=uc, n=n: e.tensor_tensor(
                    out=u[:, oc, uc:uc + n], in0=tmp[ti][:, :n], in1=tmp[ti][:, :n], op=ALU.mult),
                    reads=[("tmp", ti)], writes=[("u", oc, uc)])
        if dbg:
            for kc in range(32):
                P.store(g_u[:, kc, :], u[:, kc, 0:512], ("u", kc, 0))
        S.alias("ysb", ["h2"])
        for oc in range(8):
            wis = [wget(32 + 4 * oc + q) for q in range(4)]
            for (xfn, xkey, n, v, uc) in groups:
                b = P.nxt("psmain", 4)
                for kc in range(32):
                    wi = wis[kc // 8]
                    S.add("pe", lambda e, b=b, kc=kc, wi=wi, uc=uc, n=n: e.matmul(
                        psb[b][:, :n], lhsT=WB[wi][:, (kc % 8) * 128:(kc % 8 + 1) * 128],
                        rhs=u[:, kc, uc:uc + n], start=(kc == 0), stop=(kc == 31)),
                        reads=[("wb", wi), ("u", kc, uc)], writes=[("ps", b)])
                S.add("dve", lambda e, b=b, oc=oc, uc=uc, n=n: e.tensor_copy(out=ysb[:, oc, uc:uc + n], in_=psb[b][:, :n]),
                      reads=[("ps", b)], writes=[("ysb", oc, uc)])
        if dbg:
            for kc in range(8):
                P.store(g_y[:, kc, :], ysb[:, kc, 0:512], ("ysb", kc, 0))
        for (xfn, xkey, n, v, uc) in groups:
            yfn = lambda kc, uc=uc, n=n: ysb[:, kc, uc:uc + n]
            ykey = lambda kc, uc=uc: ("ysb", kc, uc)
            r = rms_rstd(yfn, ykey, n)
            resid_add(xfn, xkey, yfn, ykey, n, r, Gp2, ("Gp2",), v)

    def xmain(g):
        off = HO if A else 0
        return (lambda kc: X[:, kc, off + g * 512: off + (g + 1) * 512]), (lambda kc: ("X", kc, off + g * 512))

    S.alias_deps["X"] = [S.last_w[("X", kc)] for kc in range(8)]
    if A:
        S.alias_deps["XC"] = []

    if A:
        P.load(pscale[:], d_pscale, ("pscale",))
        P.load(edge[:].rearrange("p a c -> p (a c)"), d_edge, ("edge",))
        P.load(hmask[:], d_hmask, ("hmask",))
        P.load(gqk[:], d_gqk, ("gqk",))
        P.load(bada1[:], d_bada1, ("bada1",))
        P.load(g1pre[:], d_g1pre, ("g1pre",))
        for i in range(2):
            wpi = stage(d_wpool[i])
            S.add("dve", lambda e, i=i, wpi=wpi: e.tensor_copy(out=wpool[:, i * 1024:(i + 1) * 1024], in_=STG[wpi][:]),
                  reads=[("stg", wpi)], writes=[("wpool", i)])
        ada(d_wada, list(range(16)), mods, ("mods", 0), bada, ("bada",), 0)
        combine(G1, ("G1",), gains[:, 0, :], ("gains",), mods, ("mods", 0), 1, True)

        def cut(k):
            if CUT <= k and not FU:
                x1v_ = o_x1.rearrange("(c p) t -> p c t", p=128)
                for kc_ in range(8):
                    S.add("sp", lambda e, kc_=kc_: e.dma_start(out=x1v_[:, kc_, :], in_=X[:, kc_, HO:HO + T]),
                          reads=[("X", kc_, HO + g_ * 512) for g_ in range(4)], dma=True)
                    P.finals.append(S.all_ops[-1])
                if DBG:
                    P.store(g_m, mods[:].rearrange("p o v -> p (o v)"), ("mods",))
                    S.add("dve", lambda e: e.tensor_copy(out=tmp[0][:, 0:96], in_=psb[PS_MODS][:, 0:96]),
                          reads=[("ps", PS_MODS)], writes=[("tmp", 0)])
                    P.store(g_y[:, 0, 0:96], tmp[0][:, 0:96], ("tmp", 0))
                raise _Cut(P)
        def hgroups(src, skey, dst, dkey, width, v):
            segs = [(0, HO, 0), (width - HO, HO, 1)]
            c0 = HO
            while c0 < width - HO:
                n = min(512, width - HO - c0)
                segs.append((c0, n, None))
                c0 += n
            for (c0, n, mk) in segs:
                sfn = lambda kc, c0=c0, n=n: src[:, kc, c0:c0 + n]
                sk = lambda kc, c0=c0: (skey, kc, c0)
                dfn = lambda kc, c0=c0, n=n: dst[:, kc, c0:c0 + n]
                dk = lambda kc, c0=c0: (dkey, kc, c0)
                r = rms_rstd(sfn, sk, n)
                modulate(sfn, sk, n, r, G1, ("G1",), mods, ("mods", 0), 0, v, dfn, dk)
                if mk is not None:
                    for kc in range(8):
                        S.add("dve", lambda e, kc=kc, c0=c0, n=n, mk=mk: e.tensor_scalar(
                            out=dst[:, kc, c0:c0 + n], in0=dst[:, kc, c0:c0 + n], scalar1=hmask[:, mk:mk + 1], scalar2=None, op0=ALU.mult),
                            reads=[("hmask",), dk(kc)], writes=[dk(kc)])
        cut(1)
        S.alias_deps["XCs"] = [S.last_w[("XC",)]]
        hgroups(X, "X", hbuf, "hbuf", XW, 0)
        hgroups(XC, "XCs", HC, "HC", TC + 2 * HO, 1)

        def hkeys(dkey, kc, lo, hi, width):
            ks = []
            segs = [(0, HO), (width - HO, HO)]
            c0 = HO
            while c0 < width - HO:
                n = min(512, width - HO - c0)
                segs.append((c0, n)); c0 += n
            for (s0, n) in segs:
                if s0 < hi and s0 + n > lo:
                    ks.append((dkey, kc, s0))
            return ks

        def pool_group(hb, hkey, width, c0, n, v, xsrc, xkey, left_edge, right_edge):
            S.alias("diff", ["h2", "ysb"]); S.alias("ym", ["h2", "ysb"]); S.alias("pscr", ["h2", "ysb"])
            base = c0 - HO
            for g, w in enumerate((2, 4, 8, 16)):
                for kc in (2 * g, 2 * g + 1):
                    hh = lambda a, b, kc=kc: hb[:, kc, base + a: base + b]
                    hk = hkeys(hkey, kc, base, base + n + 16, width)
                    pa, pb_, pc, pS = pscr
                    eng = P.ew("pool_eng")

                    def tt(out, i0, i1, reads, writes, eng=eng):
                        S.add(eng, lambda e: e.tensor_tensor(out=out, in0=i0, in1=i1, op=ALU.add), reads=reads, writes=writes)
                    kA, kB, kC, kS = ("pscr", 0), ("pscr", 1), ("pscr", 2), ("pscr", 3)
                    if w == 2:
                        tt(pS[:, :n], hh(7, 7 + n), hh(8, 8 + n), hk, [kS])
                    else:
                        tt(pa[:, :n + 15], hh(0, n + 15), hh(1, n + 16), hk, [kA])
                        if w == 4:
                            tt(pS[:, :n], pa[:, 6:6 + n], pa[:, 8:8 + n], [kA], [kS])
                        else:
                            tt(pb_[:, :n + 13], pa[:, 0:n + 13], pa[:, 2:n + 15], [kA], [kB])
                            if w == 8:
                                tt(pS[:, :n], pb_[:, 4:4 + n], pb_[:, 8:8 + n], [kB], [kS])
                            else:
                                tt(pc[:, :n + 9], pb_[:, 0:n + 9], pb_[:, 4:n + 13], [kB], [kC])
                                tt(pS[:, :n], pc[:, 0:n], pc[:, 8:8 + n], [kC], [kS])
                    S.add("dve", lambda e, kc=kc, w=w: e.scalar_tensor_tensor(
                        out=diff[:, kc, :n], in0=pS[:, :n], scalar=1.0 / w, in1=hb[:, kc, c0:c0 + n], op0=ALU.mult, op1=ALU.subtract),
                        reads=[kS] + hk, writes=[("diff", kc)])
                    for (flag, cs, ei) in ((left_edge, 0, 0), (right_edge, n - 8, 8)):
                        if flag:
                            S.add("dve", lambda e, cs=cs, ei=ei, g=g: e.tensor_tensor(
                                out=pS[:, cs:cs + 8], in0=pS[:, cs:cs + 8], in1=edge[:, g, ei:ei + 8], op=ALU.mult),
                                reads=[kS, ("edge",)], writes=[kS])
                            S.add("dve", lambda e, cs=cs, kc=kc: e.tensor_tensor(
                                out=diff[:, kc, cs:cs + 8], in0=pS[:, cs:cs + 8], in1=hb[:, kc, c0 + cs:c0 + cs + 8], op=ALU.subtract),
                                reads=[kS] + hk, writes=[("diff", kc)])
                for ol in range(2):
                    oc = 2 * g + ol
                    b = P.nxt("psmain", 4)
                    for kl in range(2):
                        S.add("pe", lambda e, b=b, g=g, kl=kl, ol=ol: e.matmul(
                            psb[b][:, :n], lhsT=wpool[:, g * 512 + kl * 256 + ol * 128: g * 512 + kl * 256 + (ol + 1) * 128],
                            rhs=diff[:, 2 * g + kl, :n], start=(kl == 0), stop=(kl == 1)),
                            reads=[("wpool", g // 2), ("diff", 2 * g + kl)], writes=[("ps", b)])
                    S.add("dve", lambda e, b=b, oc=oc: e.tensor_scalar(
                        out=ym[:, oc, :n], in0=psb[b][:, :n], scalar1=pscale[:, oc:oc + 1], scalar2=None, op0=ALU.mult),
                        reads=[("ps", b), ("pscale",)], writes=[("ym", oc)])
            yfn = lambda kc: ym[:, kc, :n]
            ykey = lambda kc: ("ym", kc)
            r = rms_rstd(yfn, ykey, n)
            resid_add(xsrc, xkey, yfn, ykey, n, r, Gp1, ("Gp1",), v)

        cut(2)
        S.alias("hbuf", ["u"]); S.alias("HC", ["u"])
        ada(d_wada, list(range(16, 24)), mods, ("mods", 1), bada, ("bada",), 0)
        combine(Gp1, ("Gp1",), gains[:, 1, :], ("gains",), mods, ("mods", 1), 2, False)
        for g in range(8):
            c0 = HO + g * 256
            pool_group(hbuf, "hbuf", XW, c0, 256, 0,
                       lambda kc, c0=c0: X[:, kc, c0:c0 + 256], lambda kc, c0=c0: ("X", kc, HO + ((c0 - HO) // 512) * 512), g == 0, g == 7)
            ada_mm(d_wada, list(range(24 + 3 * g, 27 + 3 * g)), 0)
        ada_evac(24, 48, mods, ("mods",), bada, ("bada",), 0)
        combine(G2, ("G2",), gains[:, 2, :], ("gains",), mods, ("mods",), 4, True)
        combine(Gp2, ("Gp2",), gains[:, 3, :], ("gains",), mods, ("mods",), 5, False)
        pool_group(HC, "HC", TC + 2 * HO, HO, TC, 1,
                   lambda kc: XC[:, kc, HO:HO + TC], lambda kc: ("XCs", kc, HO), True, True)

        cut(3)
        def lat_group(g, uc):
            c0 = HO + g * 512
            return (lambda kc, c0=c0: X[:, kc, c0:c0 + 512], lambda kc, c0=c0: ("X", kc, c0), 512, 0, uc)
        ctx_group = (lambda kc: XC[:, kc, HO:HO + TC], lambda kc: ("XCs", kc, HO), TC, 1, 1024)
        mlp([lat_group(0, 0), lat_group(1, 512), ctx_group])
        mlp([lat_group(2, 0), lat_group(3, 512)])

        cut(4)
        if not FU:
            x1v = o_x1.rearrange("(c p) t -> p c t", p=128)
            for kc in range(8):
                P.S.add("sp", lambda e, kc=kc: e.dma_start(out=x1v[:, kc, :], in_=X[:, kc, HO:HO + T]),
                        reads=[("X", kc, HO + g * 512) for g in range(4)], dma=True)
                P.finals.append(S.all_ops[-1])
        qstores, kstores, vstores = [], [], []

        ada(d_wadaB if FU else d_wada1, list(range(16)), mods1, ("mods1",), bada1, ("bada1",), 0)
        combine(G1b, ("G1b",), g1pre[:], ("g1pre",), mods1, ("mods1",), 1, True)
        h1 = MID[:].bitcast(BF16)[:, 0:8 * (T + TC)].rearrange("p (c t) -> p c t", c=8)
        S.alias("h1", ["h2", "ysb", "diff", "ym", "pscr", "wpool"])
        S.alias("rope", ["u", "hbuf"])
        cosT = BIG[:, 0:T]; sinT = BIG[:, T:2 * T]
        qn = [BIG[:, 2 * T + i * 512: 2 * T + (i + 1) * 512] for i in range(2)]
        P.load(rot, d_rot, ("rope", "rot"))
        P.load(cosT, d_cos, ("rope", "cos"))
        P.load(sinT, d_sin, ("rope", "sin"))
        grp1 = [(lambda kc, g=g: X[:, kc, HO + g * 512:HO + (g + 1) * 512], lambda kc, g=g: ("X", kc, HO + g * 512), 512, 0, g * 512) for g in range(4)]
        grp1.append((lambda kc: XC[:, kc, HO:HO + TC], lambda kc: ("XCs", kc, HO), TC, 1, T))
        for (xfn, xkey, n, v, uc) in grp1:
            r = rms_rstd(xfn, xkey, n)
            modulate(xfn, xkey, n, r, G1b, ("G1b",), mods1, ("mods1",), 0, v,
                     lambda kc, uc=uc, n=n: h1[:, kc, uc:uc + n], lambda kc, uc=uc: ("h1", kc, uc))
        cut(5)
        items = []
        for hd in range(10):
            for grp in grp1:
                if hd < 8 and grp[3] == 1:
                    continue
                items.append((hd, grp))
        qkvget = wstream([d_wqkv[i] for i in range(12)], 2)

        def make_item(hd, grp):
            (xfn, xkey, n, v, uc) = grp
            isq = hd < 8
            gcol = 0 if isq else 1
            st_ = {}

            def s0():
                wi = qkvget(hd)
                b = st_["b"] = P.nxt("psmain", 4)
                for kc in range(8):
                    S.add("pe", lambda e, b=b, kc=kc, wi=wi: e.matmul(
                        psb[b][:, :n], lhsT=WB[wi][:, kc * 128:(kc + 1) * 128],
                        rhs=h1[:, kc, uc:uc + n], start=(kc == 0), stop=(kc == 7)),
                        reads=[("wb", wi), ("h1", kc, uc)], writes=[("ps", b)])

            def s1():
                b = st_["b"]
                qi = st_["qi"] = P.nxt("sq", 2)
                S.add("act", lambda e: e.activation(out=sq[qi][:, :n], in_=psb[b][:, :n], func=AF.Square),
                      reads=[("ps", b)], writes=[("sq", qi)])

            def s2():
                qi = st_["qi"]
                pb = st_["pb"] = (4, 5)[P.nxt("qkstat", 2)]
                S.add("pe", lambda e: e.matmul(psb[pb][:, :n], lhsT=ones[:], rhs=sq[qi][:, :n], start=True, stop=True),
                      reads=[("sq", qi), ("ones",)], writes=[("ps", pb)])

            def s3():
                pb = st_["pb"]
                r = st_["r"] = P.nxt("rstd", 2)
                S.add("act", lambda e: e.activation(out=rstd[r][:, :n], in_=psb[pb][:, :n], func=AF.Ln, bias=EPS, scale=1.0 / 128),
                      reads=[("ps", pb)], writes=[("rstd", r)])
                S.add("act", lambda e: e.activation(out=rstd[r][:, :n], in_=rstd[r][:, :n], func=AF.Exp, scale=-0.5),
                      reads=[("rstd", r)], writes=[("rstd", r)])

            def s4():
                b, r = st_["b"], st_["r"]
                qj = st_["qj"] = P.nxt("qn", 2)
                S.add("dve", lambda e: e.scalar_tensor_tensor(
                    out=qn[qj][:, :n], in0=psb[b][:, :n], scalar=gqk[:, gcol:gcol + 1], in1=rstd[r][:, :n], op0=ALU.mult, op1=ALU.mult),
                    reads=[("ps", b), ("gqk",), ("rstd", r)], writes=[("rope", "qn", qj)])

            def s5():
                if v != 0:
                    return
                qj = st_["qj"]
                b2 = st_["b2"] = P.nxt("psmain", 4)
                S.add("pe", lambda e: e.matmul(psb[b2][:, :n], lhsT=rot, rhs=qn[qj][:, :n], start=True, stop=True),
                      reads=[("rope", "rot"), ("rope", "qn", qj)], writes=[("ps", b2)], nosig=True)
                pe_marker([("rope", "rot"), ("rope", "qn", qj)], [("ps", b2)])

            def s6():
                if v != 0:
                    return
                qj, b2 = st_["qj"], st_["b2"]
                ti = st_["ti"] = P.nxt("tmp", 2)
                S.add("dve", lambda e: e.tensor_tensor(out=tmp[ti][:, :n], in0=psb[b2][:, :n], in1=sinT[:, uc:uc + n], op=ALU.mult),
                      reads=[("ps", b2), ("rope", "sin")], writes=[("tmp", ti)])
                S.add("pool", lambda e: e.tensor_tensor(out=qn[qj][:, :n], in0=qn[qj][:, :n], in1=cosT[:, uc:uc + n], op=ALU.mult),
                      reads=[("rope", "qn", qj), ("rope", "cos")], writes=[("rope", "qn", qj)])

            def s7():
                qj = st_["qj"]
                oi = P.nxt("ostg", 2)
                if v == 0:
                    ti = st_["ti"]
                    S.add("dve", lambda e: e.tensor_tensor(out=ostg[oi][:, :n], in0=qn[qj][:, :n], in1=tmp[ti][:, :n], op=ALU.add),
                          reads=[("rope", "qn", qj), ("tmp", ti)], writes=[("rope", "ostg", oi)])
                else:
                    S.add("dve", lambda e: e.tensor_copy(out=ostg[oi][:, :n], in_=qn[qj][:, :n]),
                          reads=[("rope", "qn", qj)], writes=[("rope", "ostg", oi)])
                dst = o_q[hd, :, uc:uc + n] if isq else o_k[hd - 8, :, uc:uc + n]
                (qstores if isq else kstores).append(P.store(dst, ostg[oi][:, :n], ("rope", "ostg", oi)))
            return [s0, s1, s2, s3, s4, s5, s6, s7]

        WAVE = 2
        for i0 in range(0, len(items), WAVE):
            batch = [make_item(*it) for it in items[i0:i0 + WAVE]]
            for si in range(8):
                for stg_ in batch:
                    stg_[si]()
        wv = [qkvget(10 + i) for i in range(2)]
        tiles = [(t0, 128) for t0 in range(0, T, 128)] + [(T, TC)]
        for (t0, m) in tiles:
            b = P.nxt("psmain", 4)
            for i in range(2):
                for kc in range(8):
                    S.add("pe", lambda e, b=b, kc=kc, t0=t0, m=m, i=i: e.matmul(
                        psb[b][:m, i * 128:(i + 1) * 128], lhsT=h1[:, kc, t0:t0 + m], rhs=WB[wv[i]][:, kc * 128:(kc + 1) * 128],
                        start=(kc == 0), stop=(kc == 7)),
                        reads=[("wb", wv[i])] + [("h1", kc, (t0 // 512) * 512)], writes=[("ps", b)])
            oi = P.nxt("ostg", 2)
            S.add("dve", lambda e, b=b, oi=oi, m=m: e.tensor_copy(out=ostg[oi][:m, :256], in_=psb[b][:m, :256]),
                  reads=[("ps", b)], writes=[("rope", "ostg", oi)])
            vstores.append(P.store(o_v[t0:t0 + m, :], ostg[oi][:m, :256], ("rope", "ostg", oi)))
        if FU:
            RG = [[0, 1, 2, 3], [4, 5, 6, 7]]
            cck = S.add("pool", lambda e: e.collective_compute("AllGather", ALU.bypass, replica_groups=RG, ins=[k_loc], outs=[k_all]),
                        writes=[("kall",)], dma=True, after=kstores)
            ccv = S.add("pool", lambda e: e.collective_compute("AllGather", ALU.bypass, replica_groups=RG, ins=[o_v], outs=[v_all]),
                        writes=[("vall",)], dma=True, after=vstores)

    if Bm:
        if FU:
            oldB = ["u", "hbuf", "HC", "rope"]
            S.alias("KV", oldB); S.alias("OTall", oldB)
            S.alias("QT", ["h1", "h2", "ysb", "diff", "ym", "pscr", "wpool"])
            P.load(gainsB[:].rearrange("p a c -> p (a c)"), d_gainsB, ("gainsB",))
        P.load(gqkrow[:], d_gqkrow, ("gqkrow",))
        S.add("dve", lambda e: e.tensor_reduce(out=c1[:, 0:2], in_=gqkrow[:].rearrange("p (a d) -> p a d", a=2),
                                                axis=mybir.AxisListType.X, op=ALU.max, apply_absolute_value=True),
              reads=[("gqkrow",)], writes=[("c1",)])
        S.add("dve", lambda e: e.scalar_tensor_tensor(out=c1[:, 2:3], in0=c1[:, 0:1], scalar=-(128.0 ** 0.5), in1=c1[:, 1:2], op0=ALU.mult, op1=ALU.mult),
              reads=[("c1",)], writes=[("c1",)])
        S.add("pool", lambda e: e.memset(ones32[:], 1.0), writes=[("ones32",)])
        S.add("pe", lambda e: e.matmul(psb[PS_MODS][:, 100:101], lhsT=ones32[:], rhs=c1[:, 2:3], start=True, stop=True),
              reads=[("ones32",), ("c1",)], writes=[("ps", PS_MODS)], nosig=True)
        pe_marker([("ones32",), ("c1",)], [("ps", PS_MODS)])
        S.add("dve", lambda e: e.tensor_copy(out=negc[:], in_=psb[PS_MODS][:, 100:101]), reads=[("ps", PS_MODS)], writes=[("negc",)])
        gkB = ("gainsB",) if FU else ("gains",)

        def mods_layer1():
            ada(d_wadaB, list(range(16, 48)), mods[:, 16:48, :], ("mods",), bada1 if FU else bada, ("bada1",) if FU else ("bada",), 16)
            combine(Gp1, ("Gp1",), gainsB[:, 1, :], gkB, mods, ("mods",), 2, False)
            combine(G2, ("G2",), gainsB[:, 2, :], gkB, mods, ("mods",), 4, True)
            combine(Gp2, ("Gp2",), gainsB[:, 3, :], gkB, mods, ("mods",), 5, False)
        SCALE = 128.0 ** -0.5
        NK = 66
        PS_S = (0, 1, 2)
        for kvh in range(2):
            if kvh == 1:
                mods_layer1()
            for part in range(4):
                if FU:
                    P.load(KT[:, part * 2112:(part + 1) * 2112], k_all[part * 256 + kvh * 128: part * 256 + (kvh + 1) * 128, :],
                           ("KV", "k", part), after=[cck])
                else:
                    P.load(KT[:, part * 2112:(part + 1) * 2112], d_k[kvh, :, part * 2112:(part + 1) * 2112], ("KV", "k", part))
            if FU:
                vv = v_all[:, kvh * 128:(kvh + 1) * 128].rearrange("(k p) d -> p k d", p=128)
            else:
                vv = d_v[kvh].rearrange("(k p) d -> p k d", p=128)
            for part in range(6):
                P.load(V[:, part * 11:(part + 1) * 11, :], vv[:, part * 11:(part + 1) * 11, :], ("KV", "v", part),
                       after=([ccv] if FU else []))
            for qg in range(4):
                qb = P.nxt("QTg", 2)
                for hl in range(4):
                    P.load(QTg[qb][:, hl, :], d_q[4 * kvh + hl, :, qg * 512:(qg + 1) * 512], ("QT", qb, hl),
                           after=(qstores if FU else []))
                for hl in range(4):
                    hh = 4 * kvh + hl
                    bo, bz = 4 + (hl % 2), 6 + (hl % 2)
                    NP_ = NK // 2

                    def s_pair(j, hl=hl, qb=qb):
                        sp_ = j % 2
                        pb_ = j % 3
                        for t_ in range(2):
                            kt = 2 * j + t_
                            part = (kt * 128) // 2112
                            part2 = (kt * 128 + 127) // 2112
                            S.add("pe", lambda e, kt=kt, t_=t_: e.matmul(psb[2 * sp_ + t_], lhsT=KT[:, kt * 128:(kt + 1) * 128],
                                                                       rhs=QTg[qb][:, hl, :], start=True, stop=True),
                                  reads=[("KV", "k", part), ("KV", "k", part2), ("QT", qb, hl)], writes=[("ps", 2 * sp_ + t_)])
                        S.add("act", lambda e: e.activation(out=Pb[pb_], in_=PSALL[:, sp_ * 1024:(sp_ + 1) * 1024], func=AF.Exp,
                                                            bias=negc[:, 0:1], scale=SCALE),
                              reads=[("ps", 2 * sp_), ("ps", 2 * sp_ + 1), ("negc",)], writes=[("Pb", pb_)])
                        S.add("dve", lambda e: e.tensor_tensor(out=Pz[pb_], in0=Pb[pb_][:, 0:512], in1=Pb[pb_][:, 512:1024], op=ALU.add),
                              reads=[("Pb", pb_)], writes=[("Pz", pb_)])
                        if j % 2 == 1:
                            pv_ = (j - 1) % 3
                            S.add("dve", lambda e: e.tensor_tensor(out=Pz[pb_], in0=Pz[pb_], in1=Pz[pv_], op=ALU.add),
                                  reads=[("Pz", pb_), ("Pz", pv_)], writes=[("Pz", pb_)])

                    def pv_pair(j, bo=bo, bz=bz):
                        sp_ = j % 2
                        pb_ = j % 3
                        for t_ in range(2):
                            kt = 2 * j + t_
                            S.add("pe", lambda e, kt=kt, t_=t_: e.matmul(psb[bo], lhsT=V[:, kt, :], rhs=Pb[pb_][:, t_ * 512:(t_ + 1) * 512],
                                                                       start=(kt == 0), stop=(kt == NK - 1)),
                                  reads=[("KV", "v", kt // 11), ("Pb", pb_)], writes=[("ps", bo)])
                        if j % 2 == 1 or j == NP_ - 1:
                            S.add("pe", lambda e: e.matmul(psb[bz], lhsT=ones[:], rhs=Pz[pb_], start=(j == 1), stop=(j == NP_ - 1)),
                                  reads=[("ones",), ("Pz", pb_)], writes=[("ps", bz)])
                    s_pair(0)
                    s_pair(1)
                    for j in range(NP_):
                        if j + 2 < NP_:
                            s_pair(j + 2)
                        pv_pair(j)
                    ti = P.nxt("tmp", 2)
                    S.add("dve", lambda e, ti=ti, bz=bz: e.reciprocal(out=tmp[ti][:], in_=psb[bz]), reads=[("ps", bz)], writes=[("tmp", ti)])
                    S.add("dve", lambda e, ti=ti, bo=bo, hh=hh, qg=qg: e.tensor_tensor(out=OTall[:, hh, qg * 512:(qg + 1) * 512], in0=psb[bo], in1=tmp[ti][:], op=ALU.mult),
                          reads=[("ps", bo), ("tmp", ti)], writes=[("OTall", hh, qg)])
        S.alias("yw", ["QT", "Pb", "Pz", "h1", "h2", "ysb", "diff", "ym", "pscr", "wpool"])
        woget = wstream([d_wo[oc] for qg in range(4) for oc in range(8)], 3)
        for qg in range(4):
            for oc in range(8):
                wi = woget(qg * 8 + oc)
                b = P.nxt("psmain", 4)
                for hh in range(8):
                    S.add("pe", lambda e, b=b, wi=wi, hh=hh, qg=qg: e.matmul(
                        psb[b], lhsT=WB[wi][:, hh * 128:(hh + 1) * 128],
                        rhs=OTall[:, hh, qg * 512:(qg + 1) * 512], start=(hh == 0), stop=(hh == 7)),
                        reads=[("wb", wi), ("OTall", hh, qg)], writes=[("ps", b)])
                S.add("dve", lambda e, b=b, oc=oc: e.tensor_copy(out=yw[:, oc, :], in_=psb[b]), reads=[("ps", b)], writes=[("yw", oc)])
            xfn, xkey = xmain(qg)
            yfn = lambda kc: yw[:, kc, :]
            ykey = lambda kc: ("yw", kc)
            if DBG and qg == 0:
                for kc in range(8):
                    P.store(g_ot[:, kc, :], OTall[:, kc, 0:512], ("OTall", kc, 0))
                    P.store(g_yw[:, kc, :], yw[:, kc, :], ("yw", kc))
                P.store(g_m, mods[:].rearrange("p o v -> p (o v)"), ("mods",))
                P.store(g_nc, negc[:], ("negc",))
            r = rms_rstd(yfn, ykey, 512)
            resid_add(xfn, xkey, yfn, ykey, 512, r, Gp1, ("Gp1",), 0)
            if DBG and qg == 0:
                for kc in range(8):
                    P.store(g_xm[:, kc, :], X[:, kc, 0:512], ("X", kc, 0))
        def lat_group(g, uc):
            xfn, xkey = xmain(g)
            return (xfn, xkey, 512, 0, uc)
        mlp([lat_group(0, 0), lat_group(1, 512)], d_winB, d_woutB)
        mlp([lat_group(2, 0), lat_group(3, 512)], d_winB, d_woutB)
        yv = o_y.rearrange("(c p) t -> p c t", p=128)
        for kc in range(8):
            S.add("sp", lambda e, kc=kc: e.dma_start(out=yv[:, kc, :], in_=X[:, kc, XOFF:XOFF + T]),
                  reads=[("X", kc, XOFF + g * 512) for g in range(4)], dma=True)
            P.finals.append(S.all_ops[-1])

    return P


def build(mode):
    try:
        P = _build(mode)
    except _Cut as c:
        P = c.args[0]
    P.S.emit_all(P.nc, P.st, final_waits=P.finals)
    P.st.close()
    return P.nc


def chunk_in(W, ncols):
    K, N = W.shape
    a = W.reshape(K // 128, 128, N // ncols, ncols)
    return np.ascontiguousarray(a.transpose(2, 1, 0, 3).reshape(N // ncols, 128, (K // 128) * ncols))


def vec8(v):
    return np.ascontiguousarray(v.reshape(-1, 128).T)


_NC_CACHE = {}
FUSED = os.environ.get("KFUSED", "0") == "1"
_DBG = {}


def get_nc(mode):
    if mode not in _NC_CACHE:
        _NC_CACHE[mode] = build(mode)
    return _NC_CACHE[mode]


def kernel(x, c, ctx, c_ctx, w_ada, b_ada, g_mix_pre, g_mix_post, g_mlp_pre, g_mlp_post,
           w_pool, pool_scale, w_qkv, g_q, g_k, w_o, w_mlp_in, w_mlp_out):
    f = lambda a: np.asarray(a, dtype=np.float32)
    x, c, ctx, c_ctx, w_ada, b_ada = map(f, (x, c, ctx, c_ctx, w_ada, b_ada))
    g_mix_pre, g_mix_post, g_mlp_pre, g_mlp_post = map(f, (g_mix_pre, g_mix_post, g_mlp_pre, g_mlp_post))
    w_pool, pool_scale, w_qkv, g_q, g_k, w_o, w_mlp_in, w_mlp_out = map(f, (w_pool, pool_scale, w_qkv, g_q, g_k, w_o, w_mlp_in, w_mlp_out))
    B, L, D = x.shape
    C = ctx.shape[1]

    def common(layer):
        wout = w_mlp_out[layer]
        a = wout.reshape(4, 8, 128, 8, 128)
        wout_c = np.ascontiguousarray(a.transpose(3, 0, 2, 1, 4).reshape(32, 128, 1024))
        return {
            "wada": chunk_in(w_ada[layer], 128),
            "bada": vec8(b_ada[layer]),
            "gains": np.ascontiguousarray(np.concatenate([vec8(g_mix_pre[layer]), vec8(g_mix_post[layer]),
                                                          vec8(g_mlp_pre[layer]), vec8(g_mlp_post[layer])], axis=1)),
            "w_in": chunk_in(w_mlp_in[layer], 128),
            "w_out": wout_c,
        }
    cmA, cmB = common(0), common(1)
    rot = np.zeros((128, 128), np.float32)
    for m in list(range(0, 32)) + list(range(64, 96)):
        rot[m + 32, m] = -1.0
        rot[m, m + 32] = 1.0
    inv_freq = np.power(np.float32(10000.0), -np.arange(0, 64, 2, dtype=np.float32) / np.float32(64))
    wpool_c = np.ascontiguousarray(w_pool[0].reshape(2, 2, 2, 128, 256).transpose(0, 3, 1, 2, 4).reshape(2, 128, 1024))
    wqkv_c = chunk_in(w_qkv[0], 128)
    wada1_c = chunk_in(w_ada[1][:, :2048], 128)
    mapsA = []
    for core in range(8):
        b, j = core // 4, core % 4
        t0 = j * T
        xp = np.zeros((T + 2 * HO, D), np.float32)
        lo, hi = max(t0 - HO, 0), min(t0 + T + HO, L)
        xp[lo - (t0 - HO):hi - (t0 - HO)] = x[b, lo:hi]
        c0 = j * TC
        cp = np.zeros((TC + 2 * HO, D), np.float32)
        lo, hi = max(c0 - HO, 0), min(c0 + TC + HO, C)
        cp[lo - (c0 - HO):hi - (c0 - HO)] = ctx[b, lo:hi]
        cond = np.stack([vec8(c[b]), vec8(c_ctx)], axis=-1).reshape(128, 16)
        edge = np.zeros((4, 16), np.float32)
        for g, w in enumerate((2, 4, 8, 16)):
            for i in range(8):
                tl = i
                cl = (tl + w - w // 2) - max(tl - w // 2, 0) if j == 0 else w
                tr = 8 - i
                cr = min(w - w // 2, tr) + w // 2 if j == 3 else w
                edge[g, i] = 1.0 / cl
                edge[g, 8 + i] = 1.0 / cr
        hmask = np.array([0.0 if j == 0 else 1.0, 0.0 if j == 3 else 1.0], np.float32)
        tt = np.arange(t0, t0 + T)
        row = (tt // 64).astype(np.float32); col = (tt % 64).astype(np.float32)
        ang = np.concatenate([row[None, :] * inv_freq[:, None]] * 2 + [col[None, :] * inv_freq[:, None]] * 2, axis=0)
        m = dict(cmA)
        m.update({
            "cond": np.ascontiguousarray(cond), "xT": np.ascontiguousarray(xp.T), "ctxT": np.ascontiguousarray(cp.T),
            "wada1": wada1_c, "bada1": vec8(b_ada[1]), "g1pre": vec8(g_mix_pre[1]),
            "w_pool": wpool_c, "pscale": vec8(pool_scale[0]),
            "edge": np.ascontiguousarray(np.broadcast_to(edge.reshape(1, 64), (128, 64))),
            "hmask": np.ascontiguousarray(np.broadcast_to(hmask.reshape(1, 2), (128, 2))),
            "w_qkv": wqkv_c, "gqk": np.ascontiguousarray(np.stack([g_q[0], g_k[0]], axis=1)),
            "cosT": np.ascontiguousarray(np.cos(ang).astype(np.float32)), "sinT": np.ascontiguousarray(np.sin(ang).astype(np.float32)),
            "rot": rot,
        })
        mapsA.append(m)
    if FUSED:
        wo_c = chunk_in(w_o[0], 128)
        gqkrow = np.ascontiguousarray(np.concatenate([g_q[0], g_k[0]]).reshape(1, 256))
        mapsF = []
        for core in range(8):
            m = dict(mapsA[core])
            m.pop("wada1")
            m.update({"wadaB": cmB["wada"], "gainsB": cmB["gains"], "w_inB": cmB["w_in"], "w_outB": cmB["w_out"],
                      "w_o": wo_c, "gqkrow": gqkrow})
            mapsF.append(m)
        resF = run_bass_kernel_spmd(get_nc("F"), mapsF, core_ids=list(range(8))).results
        out = np.empty((B, L, D), np.float32)
        for core in range(8):
            b, j = core // 4, core % 4
            out[b, j * T:(j + 1) * T] = np.asarray(resF[core]["yT"]).T
        return out
    resA = run_bass_kernel_spmd(get_nc("A"), mapsA, core_ids=list(range(8))).results
    _DBG["resA"] = resA
    wo_c = chunk_in(w_o[0], 128)
    mapsB = []
    kall, vall = {}, {}
    for b in range(B):
        ks = [np.asarray(resA[4 * b + j]["kT"]) for j in range(4)]
        vs = [np.asarray(resA[4 * b + j]["v"]) for j in range(4)]
        kall[b] = np.ascontiguousarray(np.concatenate([k[:, :, T:] for k in ks] + [k[:, :, :T] for k in ks], axis=2))
        vcat = np.concatenate([v[T:] for v in vs] + [v[:T] for v in vs], axis=0)
        vall[b] = np.ascontiguousarray(vcat.reshape(8448, 2, 128).transpose(1, 0, 2))
    for core in range(8):
        b = core // 4
        cond = np.stack([vec8(c[b]), vec8(c_ctx)], axis=-1).reshape(128, 16)
        m = dict(cmB)
        m.update({
            "cond": np.ascontiguousarray(cond), "x1T": np.asarray(resA[core]["x1T"]), "qT": np.asarray(resA[core]["qT"]),
            "kTall": kall[b], "vall": vall[b], "w_o": wo_c,
            "gqkrow": np.ascontiguousarray(np.concatenate([g_q[0], g_k[0]]).reshape(1, 256)),
        })
        mapsB.append(m)
    resB = run_bass_kernel_spmd(get_nc("B"), mapsB, core_ids=list(range(8))).results
    out = np.empty((B, L, D), np.float32)
    for core in range(8):
        b, j = core // 4, core % 4
        out[b, j * T:(j + 1) * T] = np.asarray(resB[core]["yT"]).T
    return out
```
